# Optimizing a Trainium2 kernel written in Bass

```python
import jax, jax.numpy as jnp
from jax import lax
import numpy as np

D_MODEL = 2048
BATCH = 4
SEQ = 4096
DEPTH = 2

CHUNK = 64
Q_BLOCK = 128
N_MIXERS = 2
N_MLA_LAYERS = (DEPTH + 1) // 2
N_FOX_LAYERS = DEPTH // 2

MLA_HEADS = 16
MLA_NOPE_DIM = 128
MLA_ROPE_DIM = 64
MLA_V_DIM = 128
MLA_Q_RANK = 512
MLA_KV_RANK = 512
MLA_IN_DIM = MLA_Q_RANK + MLA_KV_RANK + MLA_ROPE_DIM
ROPE_THETA = 10000.0

FOX_HEADS = 16
FOX_HEAD_DIM = 128
FOX_WIDTH = FOX_HEADS * FOX_HEAD_DIM
FOX_IN_DIM = 3 * FOX_WIDTH + FOX_HEADS

FFN_HIDDEN = -(-(8 * D_MODEL) // (3 * 256)) * 256

DEEPNORM_ALPHA = float((2 * DEPTH) ** 0.25)
DEEPNORM_BETA = float((8 * DEPTH) ** -0.25)
LN_EPS = 1e-5
RMS_EPS = 1e-6
NEG_INF = -1e30

kernel_name = "mla_fox_interleaved_deepnorm_trunk"


def layer_norm(x, g, b):
    xf = x.astype(jnp.float32)
    mu = jnp.mean(xf, axis=-1, keepdims=True)
    var = jnp.mean(jnp.square(xf - mu), axis=-1, keepdims=True)
    y = (xf - mu) * lax.rsqrt(var + LN_EPS)
    return (y * g.astype(jnp.float32) + b.astype(jnp.float32)).astype(x.dtype)


def rms_norm(x, g):
    xf = x.astype(jnp.float32)
    y = xf * lax.rsqrt(jnp.mean(jnp.square(xf), axis=-1, keepdims=True) + RMS_EPS)
    return (y * g.astype(jnp.float32)).astype(x.dtype)


def rope_tables(positions, dim):
    inv_freq = ROPE_THETA ** (-jnp.arange(0, dim, 2, dtype=jnp.float32) / dim)
    ang = positions.astype(jnp.float32)[..., None] * inv_freq
    return jnp.cos(ang), jnp.sin(ang)


def apply_rope(t, cos, sin):
    half = t.shape[-1] // 2
    t1 = t[..., :half].astype(jnp.float32)
    t2 = t[..., half:].astype(jnp.float32)
    out = jnp.concatenate([t1 * cos - t2 * sin, t2 * cos + t1 * sin], axis=-1)
    return out.astype(t.dtype)


def to_blocks(t):
    b, s = t.shape[0], t.shape[1]
    return jnp.moveaxis(t.reshape(b, s // Q_BLOCK, Q_BLOCK, *t.shape[2:]), 1, 0)


def from_blocks(t):
    n, b = t.shape[0], t.shape[1]
    return jnp.moveaxis(t, 0, 1).reshape(b, n * Q_BLOCK, *t.shape[3:])


def mla_mixer(x, cos, sin, w_in, q_norm_g, w_q_up, kv_norm_g, w_kv_up, w_o):
    b, s, _ = x.shape
    h = jnp.einsum('bsd,de->bse', x, w_in)
    c_q = h[..., :MLA_Q_RANK]
    c_kv = h[..., MLA_Q_RANK:MLA_Q_RANK + MLA_KV_RANK]
    k_rope = h[..., MLA_Q_RANK + MLA_KV_RANK:]
    q = jnp.einsum('bsr,re->bse', rms_norm(c_q, q_norm_g), w_q_up)
    q = q.reshape(b, s, MLA_HEADS, MLA_NOPE_DIM + MLA_ROPE_DIM)
    q_nope = q[..., :MLA_NOPE_DIM]
    q_rope = apply_rope(q[..., MLA_NOPE_DIM:], cos[:, :, None, :], sin[:, :, None, :])
    k_rope = apply_rope(k_rope, cos, sin)
    kv = jnp.einsum('bsr,re->bse', rms_norm(c_kv, kv_norm_g), w_kv_up)
    kv = kv.reshape(b, s, MLA_HEADS, MLA_NOPE_DIM + MLA_V_DIM)
    k_nope = kv[..., :MLA_NOPE_DIM]
    v = kv[..., MLA_NOPE_DIM:]
    scale = (MLA_NOPE_DIM + MLA_ROPE_DIM) ** -0.5
    k_chunk = jnp.arange(s) // CHUNK

    def block(args):
        i, qn, qr = args
        sc = (jnp.einsum('bqhn,bkhn->bhqk', qn, k_nope)
              + jnp.einsum('bqhr,bkr->bhqk', qr, k_rope)).astype(jnp.float32) * scale
        q_chunk = (i * Q_BLOCK + jnp.arange(Q_BLOCK)) // CHUNK
        allowed = k_chunk[None, :] <= q_chunk[:, None]
        sc = jnp.where(allowed, sc, NEG_INF)
        p = jax.nn.softmax(sc, axis=-1).astype(v.dtype)
        return jnp.einsum('bhqk,bkhv->bqhv', p, v)

    n_blk = s // Q_BLOCK
    o = lax.map(block, (jnp.arange(n_blk), to_blocks(q_nope), to_blocks(q_rope)))
    o = from_blocks(o).reshape(b, s, MLA_HEADS * MLA_V_DIM)
    return jnp.einsum('bse,ed->bsd', o, w_o)


def fox_mixer(x, w_in, b_f, w_o):
    b, s, _ = x.shape
    h = jnp.einsum('bsd,de->bse', x, w_in)
    q = h[..., :FOX_WIDTH].reshape(b, s, FOX_HEADS, FOX_HEAD_DIM)
    k = h[..., FOX_WIDTH:2 * FOX_WIDTH].reshape(b, s, FOX_HEADS, FOX_HEAD_DIM)
    v = h[..., 2 * FOX_WIDTH:3 * FOX_WIDTH].reshape(b, s, FOX_HEADS, FOX_HEAD_DIM)
    f_logit = h[..., 3 * FOX_WIDTH:].astype(jnp.float32) + b_f.astype(jnp.float32)
    log_f = jax.nn.log_sigmoid(f_logit)
    c = jnp.cumsum(log_f, axis=1)
    c_k = jnp.transpose(c, (0, 2, 1))
    scale = FOX_HEAD_DIM ** -0.5
    k_pos = jnp.arange(s)

    def block(args):
        i, qb, cb = args
        sc = jnp.einsum('bqhd,bkhd->bhqk', qb, k).astype(jnp.float32) * scale
        sc = sc + jnp.transpose(cb, (0, 2, 1))[..., None] - c_k[:, :, None, :]
        q_pos = i * Q_BLOCK + jnp.arange(Q_BLOCK)
        sc = jnp.where(k_pos[None, :] <= q_pos[:, None], sc, NEG_INF)
        p = jax.nn.softmax(sc, axis=-1).astype(v.dtype)
        return jnp.einsum('bhqk,bkhd->bqhd', p, v)

    n_blk = s // Q_BLOCK
    o = lax.map(block, (jnp.arange(n_blk), to_blocks(q), to_blocks(c)))
    o = from_blocks(o).reshape(b, s, FOX_WIDTH)
    return jnp.einsum('bse,ed->bsd', o, w_o)


def swiglu_ffn(x, w_gu, w_down):
    gu = jnp.einsum('bsd,df->bsf', x, w_gu)
    g, u = gu[..., :FFN_HIDDEN], gu[..., FFN_HIDDEN:]
    return jnp.einsum('bsf,fd->bsd', jax.nn.silu(g) * u, w_down)


def _dense(key, shape, fan_in, scale=1.0):
    return jax.random.normal(key, shape, jnp.float32) * (scale * fan_in ** -0.5)


def setup_inputs(seed: int = 0) -> dict:
    key = jax.random.key(seed)
    ks = jax.random.split(key, 24)
    NA, NF, L, D = N_MLA_LAYERS, N_FOX_LAYERS, DEPTH, D_MODEL
    beta = DEEPNORM_BETA
    x = jax.random.normal(ks[0], (BATCH, SEQ, D), jnp.float32)
    offset = jax.random.randint(ks[1], (BATCH, 1), 0, 1024, dtype=jnp.int32) * CHUNK
    positions = (offset + jnp.arange(SEQ, dtype=jnp.int32)[None, :]).astype(jnp.int32)

    mla_w_in = _dense(ks[2], (NA, D, MLA_IN_DIM), D)
    mla_q_norm_g = 1.0 + 0.02 * jax.random.normal(ks[3], (NA, MLA_Q_RANK), jnp.float32)
    mla_w_q_up = _dense(ks[4], (NA, MLA_Q_RANK, MLA_HEADS * (MLA_NOPE_DIM + MLA_ROPE_DIM)), MLA_Q_RANK)
    mla_kv_norm_g = 1.0 + 0.02 * jax.random.normal(ks[5], (NA, MLA_KV_RANK), jnp.float32)
    wk = _dense(ks[6], (NA, MLA_KV_RANK, MLA_HEADS, MLA_NOPE_DIM), MLA_KV_RANK)
    wv = _dense(ks[7], (NA, MLA_KV_RANK, MLA_HEADS, MLA_V_DIM), MLA_KV_RANK, beta)
    mla_w_kv_up = jnp.concatenate([wk, wv], axis=-1).reshape(
        NA, MLA_KV_RANK, MLA_HEADS * (MLA_NOPE_DIM + MLA_V_DIM))
    mla_w_o = _dense(ks[8], (NA, MLA_HEADS * MLA_V_DIM, D), MLA_HEADS * MLA_V_DIM, beta)

    fq = _dense(ks[9], (NF, D, FOX_WIDTH), D)
    fk = _dense(ks[10], (NF, D, FOX_WIDTH), D)
    fv = _dense(ks[11], (NF, D, FOX_WIDTH), D, beta)
    ff = _dense(ks[12], (NF, D, FOX_HEADS), D, 0.1)
    fox_w_in = jnp.concatenate([fq, fk, fv, ff], axis=-1)
    fox_b_f = 4.0 + 0.5 * jax.random.normal(ks[13], (NF, FOX_HEADS), jnp.float32)
    fox_w_o = _dense(ks[14], (NF, FOX_WIDTH, D), FOX_WIDTH, beta)

    ffn_w_gu = _dense(ks[15], (L, D, 2 * FFN_HIDDEN), D, beta)
    ffn_w_down = _dense(ks[16], (L, FFN_HIDDEN, D), FFN_HIDDEN, beta)
    ln_mix_g = 1.0 + 0.02 * jax.random.normal(ks[17], (L, D), jnp.float32)
    ln_mix_b = 0.02 * jax.random.normal(ks[18], (L, D), jnp.float32)
    ln_ffn_g = 1.0 + 0.02 * jax.random.normal(ks[19], (L, D), jnp.float32)
    ln_ffn_b = 0.02 * jax.random.normal(ks[20], (L, D), jnp.float32)
    return {
        "x": x, "positions": positions,
        "mla_w_in": mla_w_in, "mla_q_norm_g": mla_q_norm_g, "mla_w_q_up": mla_w_q_up,
        "mla_kv_norm_g": mla_kv_norm_g, "mla_w_kv_up": mla_w_kv_up, "mla_w_o": mla_w_o,
        "fox_w_in": fox_w_in, "fox_b_f": fox_b_f, "fox_w_o": fox_w_o,
        "ffn_w_gu": ffn_w_gu, "ffn_w_down": ffn_w_down,
        "ln_mix_g": ln_mix_g, "ln_mix_b": ln_mix_b, "ln_ffn_g": ln_ffn_g, "ln_ffn_b": ln_ffn_b,
    }


def reference(x, positions, mla_w_in, mla_q_norm_g, mla_w_q_up, mla_kv_norm_g, mla_w_kv_up,
              mla_w_o, fox_w_in, fox_b_f, fox_w_o, ffn_w_gu, ffn_w_down,
              ln_mix_g, ln_mix_b, ln_ffn_g, ln_ffn_b):
    cos, sin = rope_tables(positions, MLA_ROPE_DIM)
    for i in range(DEPTH):
        j = i // N_MIXERS
        if i % N_MIXERS == 0:
            mix = mla_mixer(x, cos, sin, mla_w_in[j], mla_q_norm_g[j], mla_w_q_up[j],
                            mla_kv_norm_g[j], mla_w_kv_up[j], mla_w_o[j])
        else:
            mix = fox_mixer(x, fox_w_in[j], fox_b_f[j], fox_w_o[j])
        x = layer_norm(DEEPNORM_ALPHA * x + mix, ln_mix_g[i], ln_mix_b[i])
        x = layer_norm(DEEPNORM_ALPHA * x + swiglu_ffn(x, ffn_w_gu[i], ffn_w_down[i]),
                       ln_ffn_g[i], ln_ffn_b[i])
    return x
```

```python
import contextlib
import math
import numpy as np
import ml_dtypes
import concourse.bass as bass
import concourse.mybir as mybir
from concourse.bass_utils import run_bass_kernel_spmd

F32 = mybir.dt.float32
BF16 = mybir.dt.bfloat16
I32 = mybir.dt.int32
AF = mybir.ActivationFunctionType
ALU = mybir.AluOpType

D = 2048
NCH = 16
S = 4096
B = 4
HID = 5632
NJ = 44
ALPHA = float(4 ** 0.25)
LN_EPS = 1e-5
RMS_EPS = 1e-6
NT = 2048
HG = 8
SEM_EPOCH = 30000


class Reg:
    __slots__ = ("name", "lw", "rd", "excl")

    def __init__(self, name):
        self.name = name
        self.lw = None
        self.rd = {}
        self.excl = False


class HSem:
    __slots__ = ("sem", "total", "kind")

    def __init__(self, sem, kind):
        self.sem = sem
        self.total = 0
        self.kind = kind


class DSem:
    __slots__ = ("h",)

    def __init__(self):
        self.h = None


class Prog:
    ENG = ("pe", "act", "dve", "pool", "sp")

    def __init__(self, nc, es, n_dma_sems=76):
        self.nc = nc
        self.q = {e: [] for e in self.ENG}
        self.nsig = {e: 0 for e in self.ENG}
        nes = {"pe": 6, "act": 4, "dve": 4, "pool": 3, "sp": 1}
        self.esems = {e: [es.enter_context(nc.semaphore(f"s_{e}{i}")) for i in range(n)]
                      for e, n in nes.items()}
        n_sw = 26
        self.dpool = [HSem(es.enter_context(nc.semaphore(f"s_d{i}")), "sw" if i < n_sw else "hw")
                      for i in range(n_dma_sems)]
        self.dfree = {"sw": [h for h in self.dpool if h.kind == "sw"],
                      "hw": [h for h in self.dpool if h.kind == "hw"]}
        self.seen = {e: {} for e in self.ENG}
        self.regs = []
        self.out_events = []

    def reg(self, name=""):
        r = Reg(name)
        self.regs.append(r)
        return r

    def dsem(self):
        return DSem()

    def _bind(self, ds, eng):
        kind = "sw" if eng == "pool" else "hw"
        if ds.h is None:
            ds.h = self.dfree[kind].pop()
        assert ds.h.kind == kind, "a DMA semaphore must stay on one kind of DMA queue"
        return ds.h

    def _deps(self, eng, reads, writes):
        deps = []
        for r in reads:
            if r.lw is not None:
                deps.append(r.lw)
        for w in writes:
            if w.lw is not None:
                deps.append(w.lw)
            deps.extend(w.rd.values())
        out = []
        seen = set()
        for d in deps:
            if d[0] == "c" and d[1] == "pe" and eng == "pe":
                continue
            k = (d[0], id(d[1]) if d[0] == "d" else d[1], d[2])
            if k in seen:
                continue
            seen.add(k)
            out.append(d)
        for d in out:
            if d[0] == "c":
                self.q[d[1]][d[2]]["sig"] = True
        return out

    def _commit(self, ev, reads, writes, rkey):
        for r in reads:
            r.rd[rkey] = ev
        for w in writes:
            w.lw = ev
            w.rd = {}

    def op(self, eng, fn, reads=(), writes=()):
        ex = [r for r in reads if r.excl]
        if ex:
            writes = list(writes) + ex
            reads = [r for r in reads if not r.excl]
        idx = len(self.q[eng])
        deps = self._deps(eng, reads, writes)
        self.q[eng].append({"fn": fn, "deps": deps, "sig": False, "dma": None})
        ev = ("c", eng, idx)
        self._commit(ev, reads, writes, eng)
        return ev

    def dma(self, eng, out, in_, ds, reads=(), writes=(), is_output=False):
        deps = self._deps(eng, reads, writes)
        ds = self._bind(ds, eng)
        ds.total += 16
        ev = ("d", ds, ds.total)
        self.q[eng].append({"fn": None, "deps": deps, "sig": False, "dma": (out, in_, ds)})
        self._commit(ev, reads, writes, ("d", id(ds), ds.total))
        if is_output:
            self.out_events.append(ev)
        return ev

    def dmafn(self, eng, fn, ds, reads=(), writes=(), inc=1):
        deps = self._deps(eng, reads, writes)
        ds = self._bind(ds, eng)
        ds.total += inc
        ev = ("d", ds, ds.total)
        self.q[eng].append({"fn": None, "deps": deps, "sig": False, "dma": (fn, inc, ds)})
        self._commit(ev, reads, writes, ("d", id(ds), ds.total))
        return ev

    def barrier(self):
        evs = []
        for e in self.ENG:
            idx = len(self.q[e]) - 1
            while idx >= 0 and self.q[e][idx]["fn"] is None:
                idx -= 1
            if idx >= 0:
                self.q[e][idx]["sig"] = True
                evs.append(("c", e, idx))
        for ds in self.dpool:
            if ds.total:
                evs.append(("d", ds, ds.total))
        for e in self.ENG:
            self.q[e].append({"fn": None, "deps": [d for d in evs if not (d[0] == "c" and d[1] == e)],
                              "sig": False, "dma": None})
        for r in self.regs:
            r.lw = None
            r.rd = {}

    def emit(self):
        nc = self.nc
        sigval = {}
        for e in self.ENG:
            n = self.nsig[e]
            for idx, ins in enumerate(self.q[e]):
                if ins["sig"]:
                    n += 1
                    sigval[(e, idx)] = n
            self.nsig[e] = n
            assert n < SEM_EPOCH * len(self.esems[e]), (e, n)

        def sem_of(e, n):
            k = (n - 1) // SEM_EPOCH
            return self.esems[e][k], n - k * SEM_EPOCH

        def run(e, engobj):
            seen = self.seen[e]
            for idx, ins in enumerate(self.q[e]):
                for d in ins["deps"]:
                    if d[0] == "c":
                        sem, val = sem_of(d[1], sigval[(d[1], d[2])])
                    else:
                        sem, val = d[1].sem, d[2]
                    key = id(sem)
                    if seen.get(key, 0) >= val:
                        continue
                    seen[key] = val
                    engobj.wait_ge(sem, val)
                if ins["dma"] is not None:
                    out, in_, ds = ins["dma"]
                    if callable(out):
                        out(engobj).then_inc(ds.sem, in_)
                    else:
                        engobj.dma_start(out=out, in_=in_).then_inc(ds.sem, 16)
                elif ins["fn"] is not None:
                    bi = ins["fn"](engobj)
                    if ins["sig"]:
                        sem, _ = sem_of(e, sigval[(e, idx)])
                        bi.then_inc(sem, 1)

        with nc.Block() as block:
            @block.tensor
            def _(eng):
                run("pe", eng)

            @block.scalar
            def _(eng):
                run("act", eng)

            @block.vector
            def _(eng):
                run("dve", eng)

            @block.gpsimd
            def _(eng):
                run("pool", eng)

            @block.sync
            def _(eng):
                run("sp", eng)
        self.q = {e: [] for e in self.ENG}
        for r in self.regs:
            r.lw = None
            r.rd = {}

    def phase_end(self, dsems=()):
        self.barrier()
        self.emit()
        for ds in dsems:
            if ds.h is not None:
                self.dfree[ds.h.kind].append(ds.h)
                ds.h = None

    def finish(self):
        self.q["sp"].append({"fn": None, "deps": list(self.out_events), "sig": False, "dma": None})
        self.barrier()
        self.emit()

    def mm(self, out, lhsT, rhs, start, stop, reads, writes):
        return self.op("pe", lambda e: e.matmul(out, lhsT, rhs, start=start, stop=stop), reads, writes)

    def act(self, out, in_, func, reads, writes, bias=None, scale=1.0):
        if bias is None:
            return self.op("act", lambda e: e.activation(out=out, in_=in_, func=func, scale=scale), reads, writes)
        return self.op("act", lambda e: e.activation(out=out, in_=in_, func=func, bias=bias, scale=scale),
                       reads, writes)

    def tt(self, eng, out, in0, in1, op, reads, writes):
        return self.op(eng, lambda e: e.tensor_tensor(out, in0, in1, op), reads, writes)

    def ts(self, eng, out, in0, s1, s2, op0, op1, reads, writes):
        if s2 is None:
            return self.op(eng, lambda e: e.tensor_scalar(out, in0, s1, None, op0), reads, writes)
        return self.op(eng, lambda e: e.tensor_scalar(out, in0, s1, s2, op0, op1), reads, writes)

    def stt(self, eng, out, in0, scalar, in1, op0, op1, reads, writes):
        return self.op(eng, lambda e: e.scalar_tensor_tensor(out, in0, scalar, in1, op0, op1), reads, writes)

    def cp(self, eng, out, in_, reads, writes):
        return self.op(eng, lambda e: e.tensor_copy(out, in_), reads, writes)

    def memset(self, eng, ap, val, writes):
        return self.op(eng, lambda e: e.memset(ap, val), (), writes)


class Ph:
    CNT = 0

    def __init__(self, P, es):
        self.P = P
        self.es = es
        self.ds = []
        self.n = 0

    def sb(self, shape, dt, nreg=0):
        Ph.CNT += 1
        t = self.es.enter_context(self.P.nc.sbuf_tensor(f"sb_{Ph.CNT}", list(shape), dt))
        t_r = self.P.reg() if nreg == 0 else [self.P.reg() for _ in range(nreg)]
        return t, t_r

    def ps(self, shape=(128, 512), dt=F32):
        Ph.CNT += 1
        t = self.es.enter_context(self.P.nc.psum_tensor(f"ps_{Ph.CNT}", list(shape), dt))
        r = self.P.reg()
        r.excl = True
        return t, r

    def dsem(self):
        d = self.P.dsem()
        self.ds.append(d)
        return d

    def end(self):
        self.P.phase_end(self.ds)
        self.ds = []


def ln_stats(P, psS1, psS2, mean_sb, tmp, rstd, nmr, eps):
    (s1, r_s1), (s2, r_s2) = psS1, psS2
    (m, r_m), (t, r_t), (rs, r_rs), (nm, r_nm) = mean_sb, tmp, rstd, nmr
    P.cp("dve", m[:], s1[:], [r_s1], [r_m])
    P.tt("dve", t[:], m[:], m[:], ALU.mult, [r_m], [r_t])
    P.tt("dve", t[:], s2[:], t[:], ALU.subtract, [r_s2, r_t], [r_t])
    P.ts("dve", t[:], t[:], eps, None, ALU.add, None, [r_t], [r_t])
    P.act(rs[:], t[:], AF.Sqrt, [r_t], [r_rs])
    P.op("dve", lambda e: e.reciprocal(rs[:], rs[:]), [r_rs], [r_rs])
    P.stt("dve", nm[:], m[:], -1.0, rs[:], ALU.mult, ALU.mult, [r_m, r_rs], [r_nm])


def ln_apply(P, zc, r_z, rstd, nmr, t1, t2, g_ap, b_ap, r_par, out, r_out, mul_eng="dve"):
    (rs, r_rs), (nm, r_nm), (a, r_a), (b_, r_b) = rstd, nmr, t1, t2
    P.tt(mul_eng, a[:], zc, rs[:], ALU.mult, [r_z, r_rs], [r_a])
    P.tt("pool", b_[:], a[:], nm[:], ALU.add, [r_a, r_nm], [r_b])
    P.act(out, b_[:], AF.Identity, [r_b, r_par], [r_out], bias=b_ap, scale=g_ap)


def dense_part(P, nc, oT, xT, w_o, lnp, w_gu, w_down, yT, xm, zs, nt=NT, out_is_output=True, TT=1024):
    oT_v = oT.rearrange("(c p) t -> p c t", p=128)
    xT_v = xT.rearrange("(c p) t -> p c t", p=128)
    yT_v = yT.rearrange("(c p) t -> p c t", p=128)
    xm_v = xm.rearrange("(c p) t -> p c t", p=128)
    zs_v = zs.rearrange("(c p) t -> p c t", p=128)
    wo_v = w_o.rearrange("(c p) n -> p c n", p=128)
    wgu_v = w_gu.rearrange("(c p) n -> p c n", p=128)
    wd_v = w_down.rearrange("(j p) n -> p j n", p=128)

    with contextlib.ExitStack() as es:
        ph = Ph(P, es)
        wo_sb, r_wo4 = ph.sb([128, NCH, D], BF16, nreg=4)
        for cb in range(4):
            P.dma("pool", wo_sb[:, :, cb * 512:(cb + 1) * 512], wo_v[:, :, cb * 512:(cb + 1) * 512], ph.dsem(),
                  writes=[r_wo4[cb]])
        lnp_sb, r_ln = ph.sb([128, 64], F32)
        P.dma("sp", lnp_sb[:], lnp, ph.dsem(), writes=[r_ln])
        onesD, r_1 = ph.sb([128, 128], BF16)
        P.memset("pool", onesD[:], 1.0 / D, [r_1])
        oTt = [ph.sb([128, NCH, 512], BF16) + (ph.dsem(),) for _ in range(2)]
        xr = [ph.sb([128, 512], F32) + (ph.dsem(),) for _ in range(4)]
        zz = [ph.sb([128, NCH, 512], F32, nreg=NCH) for _ in range(2)]
        zb = [ph.sb([128, 512], BF16) for _ in range(3)]
        sq = [ph.sb([128, 512], BF16) for _ in range(3)]
        psY = [ph.ps() for _ in range(2)]
        psS1, psS2 = ph.ps(), ph.ps()
        mean_sb, tmp, rstd, nmr = [ph.sb([128, 512], F32) for _ in range(4)]
        t1 = [ph.sb([128, 512], F32) for _ in range(2)]
        t2 = [ph.sb([128, 512], F32) for _ in range(2)]
        yo = [ph.sb([128, 512], F32) + (ph.dsem(),) for _ in range(3)]
        n = 0
        def load_o(tt):
            o, r_o, d_o = oTt[tt % 2]
            P.dma("sp", o[:], oT_v[:, :, tt * 512:(tt + 1) * 512], d_o, writes=[r_o])

        load_o(0)

        def norm_gen(tt, z, r_z):
            tok = slice(tt * 512, (tt + 1) * 512)
            for oc in range(NCH):
                k = tt * NCH + oc
                y_, r_y, d_y = yo[k % 3]
                ln_apply(P, z[:, oc, :], r_z[oc], rstd, nmr, t1[k % 2], t2[k % 2],
                         lnp_sb[:, oc:oc + 1], lnp_sb[:, 16 + oc:17 + oc], r_ln, y_[:], r_y, mul_eng="pool")
                P.dma("act", xm_v[:, oc, tok], y_[:], d_y, reads=[r_y])
                yield

        gen = iter(())
        for tt in range(nt // 512):
            tok = slice(tt * 512, (tt + 1) * 512)
            o, r_o, d_o = oTt[tt % 2]
            z, r_z = zz[tt % 2]
            if tt + 1 < nt // 512:
                load_o(tt + 1)
            pend = None
            for oc in range(NCH):
                ps, r_ps = psY[oc % 2]
                for c in range(NCH):
                    P.mm(ps[:], wo_sb[:, c, oc * 128:(oc + 1) * 128], o[:, c, :], c == 0, c == NCH - 1,
                         [r_wo4[oc // 4], r_o], [r_ps])
                if pend is not None:
                    pend()
                x_, r_x, d_x = xr[n % 4]
                P.dma("sp", x_[:], xT_v[:, oc, tok], d_x, writes=[r_x])
                P.stt("dve", z[:, oc, :], x_[:], ALPHA, ps[:], ALU.mult, ALU.add, [r_x, r_ps], [r_z[oc]])
                zb_, r_zb = zb[n % 3]
                sq_, r_sq = sq[n % 3]
                P.cp("dve", zb_[:], z[:, oc, :], [r_z[oc]], [r_zb])
                P.act(sq_[:], z[:, oc, :], AF.Square, [r_z[oc]], [r_sq])

                def pend(oc=oc, zb_=zb_, r_zb=r_zb, sq_=sq_, r_sq=r_sq):
                    P.mm(psS1[0][:], onesD[:], zb_[:], oc == 0, oc == NCH - 1, [r_1, r_zb], [psS1[1]])
                    P.mm(psS2[0][:], onesD[:], sq_[:], oc == 0, oc == NCH - 1, [r_1, r_sq], [psS2[1]])
                n += 1
                next(gen, None)
            pend()
            for _ in gen:
                pass
            ln_stats(P, psS1, psS2, mean_sb, tmp, rstd, nmr, LN_EPS)
            gen = norm_gen(tt, z, r_z)
        for _ in gen:
            pass
        ph.end()

    NTT = TT // 512
    with contextlib.ExitStack() as es0:
        ph0 = Ph(P, es0)
        hT, r_h = ph0.sb([128, NJ, TT], BF16, nreg=NJ * NTT)
        psS1 = [ph0.ps() for _ in range(NTT)]
        psS2 = [ph0.ps() for _ in range(NTT)]
        rstd = [ph0.sb([128, 512], F32) for _ in range(NTT)]
        nmr = [ph0.sb([128, 512], F32) for _ in range(NTT)]
        lnp_sb, r_ln = ph0.sb([128, 64], F32)
        onesD, r_1 = ph0.sb([128, 128], BF16)
        xb, r_xb = ph0.sb([128, NCH, TT], BF16)
        d_xb = ph0.dsem()
        wdA = ph0.sb([128, NJ, 256], BF16) + (ph0.dsem(),)

        def load_xb(T0x):
            for c4 in range(4):
                P.dma("pool", xb[:, 4 * c4:4 * c4 + 4, :], xm_v[:, 4 * c4:4 * c4 + 4, T0x:T0x + TT], d_xb,
                      writes=[r_xb])

        def load_wd_to(slot, oc2):
            w_, r_w, d_w = slot
            P.dma("pool", w_[:, 0:NJ // 2, :], wd_v[:, 0:NJ // 2, oc2 * 256:(oc2 + 1) * 256], d_w, writes=[r_w])
            P.dma("pool", w_[:, NJ // 2:NJ, :], wd_v[:, NJ // 2:NJ, oc2 * 256:(oc2 + 1) * 256], d_w, writes=[r_w])

        first = True
        prevT0 = None

        def f_pass(ph, T0p):
            zin = [ph.sb([128, 512], F32) + (ph.dsem(),) for _ in range(3)]
            t1 = [ph.sb([128, 512], F32) for _ in range(2)]
            t2 = [ph.sb([128, 512], F32) for _ in range(2)]
            yo = [ph.sb([128, 512], F32) + (ph.dsem(),) for _ in range(3)]
            n = 0
            for oc in range(NCH):
                for tt in range(NTT):
                    gtok = slice(T0p + tt * 512, T0p + (tt + 1) * 512)
                    zi, r_zi, d_zi = zin[n % 3]
                    P.dma("sp", zi[:], zs_v[:, oc, gtok], d_zi, writes=[r_zi])
                    y_, r_y, d_y = yo[n % 3]
                    ln_apply(P, zi[:], r_zi, rstd[tt], nmr[tt], t1[n % 2], t2[n % 2],
                             lnp_sb[:, 32 + oc:33 + oc], lnp_sb[:, 48 + oc:49 + oc], r_ln, y_[:], r_y)
                    P.dma("act", yT_v[:, oc, gtok], y_[:], d_y, reads=[r_y], is_output=out_is_output)
                    n += 1
                    yield

        for T0 in range(0, nt, TT):
            with contextlib.ExitStack() as es:
                ph = Ph(P, es)
                if first:
                    P.dma("sp", lnp_sb[:], lnp, ph.dsem(), writes=[r_ln])
                    P.memset("pool", onesD[:], 1.0 / D, [r_1])
                    first = False
                    load_xb(T0)
                wg = [ph.sb([128, NCH, 256], BF16) + (ph.dsem(),) for _ in range(2)]
                wu = [ph.sb([128, NCH, 256], BF16) + (ph.dsem(),) for _ in range(2)]
                sg = [ph.sb([128, 512], F32) for _ in range(2)]
                psG = [ph.ps() for _ in range(2)]
                psU = [ph.ps() for _ in range(2)]
                n = 0
                fgen = f_pass(ph, prevT0) if prevT0 is not None else iter(())
                def load_gu(j2):
                    g_, r_g, d_g = wg[j2 % 2]
                    u_, r_u, d_u = wu[j2 % 2]
                    P.dma("pool", g_[:], wgu_v[:, :, j2 * 256:(j2 + 1) * 256], d_g, writes=[r_g])
                    P.dma("pool", u_[:], wgu_v[:, :, HID + j2 * 256:HID + (j2 + 1) * 256], d_u, writes=[r_u])

                load_gu(0)
                load_gu(1)
                load_wd_to(wdA, 0)
                for j2 in range(NJ // 2):
                    g_, r_g, d_g = wg[j2 % 2]
                    u_, r_u, d_u = wu[j2 % 2]
                    if 1 <= j2 and j2 + 1 < NJ // 2:
                        load_gu(j2 + 1)
                    for _ in range(2):
                        next(fgen, None)
                    for jj in range(2):
                        j = 2 * j2 + jj
                        for tt in range(NTT):
                            tok = slice(tt * 512, (tt + 1) * 512)
                            pg, r_pg = psG[n % 2]
                            pu, r_pu = psU[n % 2]
                            s_, r_s = sg[n % 2]
                            for c in range(NCH):
                                P.mm(pg[:], g_[:, c, jj * 128:(jj + 1) * 128], xb[:, c, tok], c == 0, c == NCH - 1,
                                     [r_g, r_xb], [r_pg])
                            for c in range(NCH):
                                P.mm(pu[:], u_[:, c, jj * 128:(jj + 1) * 128], xb[:, c, tok], c == 0, c == NCH - 1,
                                     [r_u, r_xb], [r_pu])
                            P.act(s_[:], pg[:], AF.Silu, [r_pg], [r_s])
                            P.tt("dve", hT[:, j, tok], s_[:], pu[:], ALU.mult, [r_s, r_pu], [r_h[j * NTT + tt]])
                            n += 1
                for _ in fgen:
                    pass
                ph.end()
            with contextlib.ExitStack() as es:
                ph = Ph(P, es)
                wd = [wdA, ph.sb([128, NJ, 256], BF16) + (ph.dsem(),)]
                xres = [ph.sb([128, 512], F32) + (ph.dsem(),) for _ in range(2)]
                zt = [ph.sb([128, 512], F32) + (ph.dsem(),) for _ in range(3)]
                zb = [ph.sb([128, 512], BF16) for _ in range(2)]
                sq = [ph.sb([128, 512], BF16) for _ in range(2)]
                psY = [ph.ps() for _ in range(2)]
                mean_sb, tmp = [ph.sb([128, 512], F32) for _ in range(2)]
                n = 0
                pend = None
                zb = zb + [ph.sb([128, 512], BF16)]
                sq = sq + [ph.sb([128, 512], BF16)]
                for oc2 in range(NCH // 2):
                    w_, r_w, d_w = wd[oc2 % 2]
                    if oc2 + 1 < NCH // 2:
                        load_wd_to(wd[(oc2 + 1) % 2], oc2 + 1)
                    if oc2 == 0 and T0 + TT < nt:
                        load_xb(T0 + TT)
                    for ocs in range(2):
                        oc = 2 * oc2 + ocs
                        for tt in range(NTT):
                            tok = slice(tt * 512, (tt + 1) * 512)
                            gtok = slice(T0 + tt * 512, T0 + (tt + 1) * 512)
                            ps, r_ps = psY[n % 2]
                            for j in range(NJ):
                                P.mm(ps[:], w_[:, j, ocs * 128:(ocs + 1) * 128], hT[:, j, tok], j == 0, j == NJ - 1,
                                     [r_w, r_h[j * NTT + tt]], [r_ps])
                            if pend is not None:
                                pend()
                            x_, r_x, d_x = xres[n % 2]
                            P.dma("sp", x_[:], xm_v[:, oc, gtok], d_x, writes=[r_x])
                            z_, r_zt, d_z = zt[n % 3]
                            P.stt("dve", z_[:], x_[:], ALPHA, ps[:], ALU.mult, ALU.add, [r_x, r_ps], [r_zt])
                            P.dma("sp", zs_v[:, oc, gtok], z_[:], d_z, reads=[r_zt])
                            zb_, r_zb = zb[n % 3]
                            sq_, r_sq = sq[n % 3]
                            P.cp("dve", zb_[:], z_[:], [r_zt], [r_zb])
                            P.act(sq_[:], z_[:], AF.Square, [r_zt], [r_sq])

                            def pend(oc=oc, tt=tt, zb_=zb_, r_zb=r_zb, sq_=sq_, r_sq=r_sq):
                                P.mm(psS1[tt][0][:], onesD[:], zb_[:], oc == 0, oc == NCH - 1, [r_1, r_zb],
                                     [psS1[tt][1]])
                                P.mm(psS2[tt][0][:], onesD[:], sq_[:], oc == 0, oc == NCH - 1, [r_1, r_sq],
                                     [psS2[tt][1]])
                            n += 1
                pend()
                for tt in range(NTT):
                    ln_stats(P, psS1[tt], psS2[tt], mean_sb, tmp, rstd[tt], nmr[tt], LN_EPS)
                ph.end()
            prevT0 = T0
        with contextlib.ExitStack() as es:
            ph = Ph(P, es)
            for _ in f_pass(ph, prevT0):
                pass
            ph.end()
        ph0.end()


def chunk_layout(v):
    return np.ascontiguousarray(v.reshape(-1, 128).T)


def ln_params(g1, b1, g2, b2):
    return np.ascontiguousarray(np.concatenate(
        [chunk_layout(g1), chunk_layout(b1), chunk_layout(g2), chunk_layout(b2)], axis=1).astype(np.float32))


_CACHE = {}


def attn_core(P, ph, nheads, load_head, scale, maskc, oT_v, kb_d, fox=None, qt0=0):
    NQT = S // 512
    LA = 3
    NPT = 7
    pT = [ph.sb([128, 512], BF16) for _ in range(NPT)]
    psS = [ph.ps() for _ in range(4)]
    psO = [ph.ps() for _ in range(2)]
    psR = [ph.ps() for _ in range(2)]
    ones, r_ones = ph.sb([128, 128], BF16)
    P.memset("pool", ones[:], 1.0, [r_ones])
    mk, r_mk = ph.sb([128, 4, 512], BF16)
    P.dma("sp", mk[:], maskc, ph.dsem(), writes=[r_mk])
    rinv = [ph.sb([128, 512], F32) for _ in range(2)]
    ost = [ph.sb([128, S], BF16) + (ph.dsem(),) for _ in range(2)]
    kb, r_kb = ph.sb([128, 32], F32)
    P.dma("sp", kb[:], kb_d, ph.dsem(), writes=[r_kb])
    if fox is not None:
        cneg, r_cneg, cref, r_cref = fox
        bia = [ph.sb([128, 32], F32) for _ in range(2)]
    tiles = []
    nq = 0
    for h in range(nheads):
        for qt in range(qt0, NQT):
            nk = 4 * (qt + 1)
            for kt in range(nk):
                tiles.append((h, qt, kt, nk, nq))
            nq += 1
    heads = {0: load_head(0, 0)}

    def issue_S(i):
        h, qt, kt, nk, nq = tiles[i]
        hd = heads[h]
        if fox is not None and kt == 0:
            b_, r_b = bia[nq % 2]
            P.ts("dve", b_[:, 0:nk], cneg[:, 0:nk, h], cref[:, qt, h:h + 1], None, ALU.subtract, None,
                 [r_cneg, r_cref], [r_b])
            if qt >= NQT // 2:
                P.tt("dve", b_[:, 0:nk], b_[:, 0:nk], kb[:, 0:nk], ALU.add, [r_b, r_kb], [r_b])
        ps, r_ps = psS[i % 4]
        parts = hd["parts"]
        j = kt - 4 * qt
        lo = 128 * j if j > 0 else 0
        for k, (kf, qf, regs) in enumerate(parts):
            P.mm(ps[:, lo:512], kf(kt), qf(qt)[:, lo:512], k == 0, k == len(parts) - 1, regs, [r_ps])
        p_, r_p = pT[i % NPT]
        if fox is not None:
            b_, r_b = bia[nq % 2]
            P.act(p_[:, lo:512], ps[:, lo:512], AF.Exp, [r_ps, r_b], [r_p], bias=b_[:, kt:kt + 1], scale=scale)
        elif qt >= NQT // 2:
            P.act(p_[:, lo:512], ps[:, lo:512], AF.Exp, [r_ps, r_kb], [r_p], bias=kb[:, kt:kt + 1], scale=scale)
        else:
            P.act(p_[:, lo:512], ps[:, lo:512], AF.Exp, [r_ps], [r_p], scale=scale)
        if j >= 0:
            P.tt("pool", p_[:, lo:lo + 128], p_[:, lo:lo + 128], mk[:, j, lo:lo + 128], ALU.mult, [r_p, r_mk], [r_p])

    def issue_PV(i):
        h, qt, kt, nk, nq = tiles[i]
        if qt == qt0 and kt == 0 and h + 1 < nheads:
            heads[h + 1] = load_head(h + 1, (h + 1) % 2)
        hd = heads[h]
        p_, r_p = pT[i % NPT]
        po, r_po = psO[nq % 2]
        vt, r_v = hd["V"]
        j = kt - 4 * qt
        lo = 128 * j if j > 0 else 0
        P.op("pe", lambda e, po=po, vt=vt, p_=p_, kt=kt, nk=nk, lo=lo: e.matmul(
            po[:, lo:512], vt[:, kt, :], p_[:, lo:512], start=(kt == 0), stop=(kt == nk - 1),
            skip_group_check=True), [r_v, r_p], [r_po])
        pr, r_pr = psR[nq % 2]
        P.op("pe", lambda e, pr=pr, p_=p_, kt=kt, nk=nk, lo=lo: e.matmul(
            pr[:, lo:512], ones[:], p_[:, lo:512], start=(kt == 0), stop=(kt == nk - 1),
            skip_group_check=True), [r_ones, r_p], [r_pr])
        if kt == nk - 1:
            ri, r_ri = rinv[nq % 2]
            o_, r_o, d_o = ost[h % 2]
            P.op("dve", lambda e, ri=ri, pr=pr: e.reciprocal(ri[:], pr[:]), [r_pr], [r_ri])
            P.tt("dve", o_[:, qt * 512:(qt + 1) * 512], po[:], ri[:], ALU.mult, [r_po, r_ri], [r_o])
            if qt == NQT - 1:
                P.dma("sp", oT_v[:, h, qt0 * 512:S], o_[:, qt0 * 512:S], d_o, reads=[r_o])

    N = len(tiles)
    for i in range(N + LA):
        if i < N:
            issue_S(i)
        if i - LA >= 0:
            issue_PV(i - LA)


TWO_PI = 2.0 * math.pi
CW1 = 6.28125
CW2 = float(np.float32(TWO_PI - 6.28125))
CW3 = float(TWO_PI - 6.28125 - float(np.float32(TWO_PI - 6.28125)))
MAGIC = 12582912.0
PI_SAFE = 3.1415925


def rope_tables(P, pht, posr, invf):
    cosT, r_cos = pht.sb([64, S], F32)
    sinT, r_sin = pht.sb([64, S], F32)
    with contextlib.ExitStack() as es:
        ph = Ph(P, es)
        posi, r_pi = ph.sb([64, S], I32)
        P.dma("sp", posi[:], posr, ph.dsem(), writes=[r_pi])
        ivf, r_iv = ph.sb([64, 1], F32)
        P.dma("sp", ivf[:], invf, ph.dsem(), writes=[r_iv])
        CB = 1024
        ang, r_a = ph.sb([64, CB], F32)
        kk, r_k = ph.sb([64, CB], F32)
        rr, r_r = ph.sb([64, CB], F32)
        yy, r_y = ph.sb([64, CB], F32)
        mm_, r_m = ph.sb([64, CB], F32)
        for cb in range(S // CB):
            sl = slice(cb * CB, (cb + 1) * CB)
            P.cp("dve", ang[:], posi[:, sl], [r_pi], [r_a])
            P.ts("dve", ang[:], ang[:], ivf[:, 0:1], None, ALU.mult, None, [r_a, r_iv], [r_a])
            P.ts("dve", kk[:], ang[:], 1.0 / TWO_PI, None, ALU.mult, None, [r_a], [r_k])
            P.ts("dve", kk[:], kk[:], MAGIC, None, ALU.add, None, [r_k], [r_k])
            P.ts("dve", kk[:], kk[:], -MAGIC, None, ALU.add, None, [r_k], [r_k])
            P.stt("dve", rr[:], kk[:], -CW1, ang[:], ALU.mult, ALU.add, [r_k, r_a], [r_r])
            P.stt("dve", rr[:], kk[:], -CW2, rr[:], ALU.mult, ALU.add, [r_k, r_r], [r_r])
            P.stt("dve", rr[:], kk[:], -CW3, rr[:], ALU.mult, ALU.add, [r_k, r_r], [r_r])
            for shift, dst, r_dst in ((0.0, sinT, r_sin), (math.pi / 2, cosT, r_cos)):
                P.ts("dve", yy[:], rr[:], shift, None, ALU.add, None, [r_r], [r_y])
                P.ts("dve", mm_[:], yy[:], math.pi, -TWO_PI, ALU.is_gt, ALU.mult, [r_y], [r_m])
                P.tt("dve", yy[:], yy[:], mm_[:], ALU.add, [r_y, r_m], [r_y])
                P.ts("dve", mm_[:], yy[:], -math.pi, TWO_PI, ALU.is_lt, ALU.mult, [r_y], [r_m])
                P.tt("dve", yy[:], yy[:], mm_[:], ALU.add, [r_y, r_m], [r_y])
                P.ts("dve", yy[:], yy[:], -PI_SAFE, PI_SAFE, ALU.max, ALU.min, [r_y], [r_y])
                P.act(dst[:, sl], yy[:], AF.Sin, [r_y], [r_dst])
        ph.end()

    return cosT, r_cos, sinT, r_sin


def mla_latents(P, L):
    work = L["work"]
    win_sb = L["win_sb"]
    r_win = L["r_win"]
    winr = L["winr"]
    r_winr = L["r_winr"]
    x_ = L["x_"]
    r_x = L["r_x"]
    lat = L["lat"]
    r_lat = L["r_lat"]
    sqb = L["sqb"]
    psSq = L["psSq"]
    psSkv = L["psSkv"]
    ones5 = L["ones5"]
    r_15 = L["r_15"]
    psA = L["psA"]
    psB = L["psB"]
    ra = L["ra"]
    rb = L["rb"]
    cosT = L["cosT"]
    r_cos = L["r_cos"]
    sinT = L["sinT"]
    r_sin = L["r_sin"]
    tok = L["tok"]
    krs = L["krs"]
    r_krs = L["r_krs"]
    d_krs = L["d_krs"]
    KR = L["KR"]
    rq = L["rq"]
    r_rq = L["r_rq"]
    rkv = L["rkv"]
    r_rkv = L["r_rkv"]
    tq = L["tq"]
    r_tq = L["r_tq"]
    cqn = L["cqn"]
    r_cqn = L["r_cqn"]
    ckvn = L["ckvn"]
    r_ckvn = L["r_ckvn"]
    gq = L["gq"]
    r_gq = L["r_gq"]
    gkv = L["gkv"]
    r_gkv = L["r_gkv"]
    CQ_v = L["CQ_v"]
    CKV_v = L["CKV_v"]
    d_cqn = L["d_cqn"]
    d_ckvn = L["d_ckvn"]
    nw = L["nw"]
    nr = L["nr"]
    for e in range(8):
        ps, r_ps = work[nw % 4]
        nw += 1
        for c in range(NCH):
            P.mm(ps[:], win_sb[:, c, e * 128:(e + 1) * 128], x_[:, c, :], c == 0, c == NCH - 1,
                 [r_win, r_x], [r_ps])
        P.cp("dve", lat[:, e, :], ps[:], [r_ps], [r_lat[e]])
        s_, r_s = sqb[e % 2]
        P.act(s_[:], ps[:], AF.Square, [r_ps], [r_s])
        pst = psSq if e < 4 else psSkv
        P.mm(pst[0][:], ones5[:], s_[:], e % 4 == 0, e % 4 == 3, [r_15, r_s], [pst[1]])
    for c in range(NCH):
        P.mm(psA[0][0:64, :], win_sb[:, c, 1024:1088], x_[:, c, :], c == 0, c == NCH - 1,
             [r_win, r_x], [psA[1]])
    for c in range(NCH):
        P.mm(psB[0][0:64, :], winr[:, c, :], x_[:, c, :], c == 0, c == NCH - 1, [r_winr, r_x], [psB[1]])
    a_, r_a = ra[nr % 2]
    b_, r_b = rb[nr % 2]
    nr += 1
    P.tt("dve", a_[:], psA[0][0:64, :], cosT[:, tok], ALU.mult, [psA[1], r_cos], [r_a])
    P.tt("dve", b_[:], psB[0][0:64, :], sinT[:, tok], ALU.mult, [psB[1], r_sin], [r_b])
    P.tt("pool", krs[:], a_[:], b_[:], ALU.add, [r_a, r_b], [r_krs])
    P.dma("sp", KR[:, tok], krs[:], d_krs, reads=[r_krs])
    for pst, rdst, r_rdst in ((psSq, rq, r_rq), (psSkv, rkv, r_rkv)):
        P.ts("dve", tq[:], pst[0][:], RMS_EPS, None, ALU.add, None, [pst[1]], [r_tq])
        P.act(rdst[:], tq[:], AF.Sqrt, [r_tq], [r_rdst])
        P.op("dve", lambda e, rdst=rdst: e.reciprocal(rdst[:], rdst[:]), [r_rdst], [r_rdst])
    for e in range(4):
        P.stt("dve", cqn[:, e, :], lat[:, e, :], gq[:, e:e + 1], rq[:], ALU.mult, ALU.mult,
              [r_lat[e], r_gq, r_rq], [r_cqn])
    for e in range(4):
        P.stt("dve", ckvn[:, e, :], lat[:, 4 + e, :], gkv[:, e:e + 1], rkv[:], ALU.mult, ALU.mult,
              [r_lat[4 + e], r_gkv, r_rkv], [r_ckvn])

    P.dma("sp", CQ_v[:, :, tok], cqn[:], d_cqn, reads=[r_cqn])
    P.dma("sp", CKV_v[:, :, tok], ckvn[:], d_ckvn, reads=[r_ckvn])
    return nw, nr


def mla_part(P, nc, xT, tabs, w_in, qg, kvg, w_q, w_kv, maskc, kb_d, oT, QN, QR, KN, KR, V, CQ, CKV, first):
    xT_v = xT.rearrange("(c p) t -> p c t", p=128)
    win_v = w_in.rearrange("(c p) n -> p c n", p=128)
    wq_v = w_q.rearrange("(c p) (h e) -> p c h e", p=128, h=HG)
    wkv_v = w_kv.rearrange("(c p) (h t d) -> p c h t d", p=128, h=HG, t=2)
    QN_v = QN.rearrange("h p t -> p h t")
    QR_v = QR.rearrange("h p t -> p h t")
    KN_v = KN.rearrange("h p t -> p h t")
    V_v = V.rearrange("(n p) e -> p n e", p=128)
    oT_v = oT.rearrange("(h p) t -> p h t", p=128)
    scale = float((128 + 64) ** -0.5)

    with contextlib.ExitStack() as es0:
        ph0 = Ph(P, es0)
        cosT, r_cos, sinT, r_sin = tabs
        with contextlib.ExitStack() as es:
            ph = Ph(P, es)
            CQ_v = CQ.rearrange("e p t -> p e t")
            CKV_v = CKV.rearrange("e p t -> p e t")
            if first:
                win_sb, r_win = ph.sb([128, NCH, 1088], BF16)
                d_w = ph.dsem()
                for c4 in range(4):
                    P.dma("pool", win_sb[:, 4 * c4:4 * c4 + 4, :], win_v[:, 4 * c4:4 * c4 + 4, :], d_w,
                          writes=[r_win])
                winr, r_winr = ph.sb([128, NCH, 64], BF16)
                P.ts("dve", winr[:, :, 0:32], win_sb[:, :, 1056:1088], -1.0, None, ALU.mult, None, [r_win], [r_winr])
                P.cp("dve", winr[:, :, 32:64], win_sb[:, :, 1024:1056], [r_win], [r_winr])
            wqn, r_wqn = ph.sb([128, 4, HG, 128], BF16)
            wqr, r_wqr = ph.sb([128, 4, HG, 64], BF16)
            wqrot, r_wqrot = ph.sb([128, 4, HG, 64], BF16)
            wk, r_wk = ph.sb([128, 4, HG, 128], BF16)
            wv, r_wv = ph.sb([128, 4, HG, 128], BF16)
            gq, r_gq = ph.sb([128, 4], F32)
            gkv, r_gkv = ph.sb([128, 4], F32)
            P.dma("sp", gq[:], qg, ph.dsem(), writes=[r_gq])
            P.dma("sp", gkv[:], kvg, ph.dsem(), writes=[r_gkv])
            ones5, r_15 = ph.sb([128, 128], BF16)
            P.memset("pool", ones5[:], 1.0 / 512, [r_15])
            if first:
                xb = [ph.sb([128, NCH, 512], BF16) + (ph.dsem(),) for _ in range(2)]
                lat, r_lat = ph.sb([128, 8, 512], F32, nreg=8)
                sqb = [ph.sb([128, 512], BF16) for _ in range(2)]
            nsl = 1 if first else 2
            cq2 = [ph.sb([128, 4, 512], BF16) + (ph.dsem(),) for _ in range(nsl)]
            ckv2 = [ph.sb([128, 4, 512], BF16) + (ph.dsem(),) for _ in range(nsl)]
            rq, r_rq = ph.sb([128, 512], F32)
            rkv, r_rkv = ph.sb([128, 512], F32)
            tq, r_tq = ph.sb([128, 512], F32)
            ra = [ph.sb([64, 512], F32) for _ in range(2)]
            rb = [ph.sb([64, 512], F32) for _ in range(2)]
            krs, r_krs = ph.sb([64, 512], BF16)
            d_krs = ph.dsem()
            qns, r_qns = ph.sb([128, HG, 512], BF16)
            qrs, r_qrs = ph.sb([64, HG, 512], BF16)
            kns, r_kns = ph.sb([128, HG, 512], BF16)
            vs, r_vs = ph.sb([128, 4, HG * 128], BF16)
            d_qns, d_qrs, d_kns, d_vs = ph.dsem(), ph.dsem(), ph.dsem(), ph.dsem()
            work = [ph.ps() for _ in range(4)]
            psSq, psSkv = ph.ps(), ph.ps()
            psA, psB = ph.ps(), ph.ps()
            nw = 0
            nr = 0
            def load_x(tt):
                if first:
                    x_, r_x, d_x = xb[tt % 2]
                    P.dma("pool", x_[:], xT_v[:, :, tt * 512:(tt + 1) * 512], d_x, writes=[r_x])
                else:
                    c_, r_c, d_c = cq2[tt % nsl]
                    P.dma("sp", c_[:], CQ_v[:, :, tt * 512:(tt + 1) * 512], d_c, writes=[r_c])
                    c_, r_c, d_c = ckv2[tt % nsl]
                    P.dma("sp", c_[:], CKV_v[:, :, tt * 512:(tt + 1) * 512], d_c, writes=[r_c])

            load_x(0)
            d1, d2, d3, d4 = ph.dsem(), ph.dsem(), ph.dsem(), ph.dsem()
            for c in range(4):
                P.dma("pool", wqn[:, c, :, :], wq_v[:, c, :, 0:128], d1, writes=[r_wqn])
                P.dma("pool", wqr[:, c, :, :], wq_v[:, c, :, 128:192], d2, writes=[r_wqr])
                P.dma("pool", wk[:, c, :, :], wkv_v[:, c, :, 0, :], d3, writes=[r_wk])
                P.dma("pool", wv[:, c, :, :], wkv_v[:, c, :, 1, :], d4, writes=[r_wv])
            for c in range(4):
                P.ts("dve", wqrot[:, c, :, 0:32], wqr[:, c, :, 32:64], -1.0, None, ALU.mult, None, [r_wqr], [r_wqrot])
                P.cp("dve", wqrot[:, c, :, 32:64], wqr[:, c, :, 0:32], [r_wqr], [r_wqrot])
            for tt in range(S // 512):
                tok = slice(tt * 512, (tt + 1) * 512)
                cqn, r_cqn, d_cqn = cq2[tt % nsl]
                ckvn, r_ckvn, d_ckvn = ckv2[tt % nsl]
                if tt + 1 < S // 512:
                    load_x(tt + 1)
                if first:
                    x_, r_x, d_x = xb[tt % 2]
                    nw, nr = mla_latents(P, locals())
                for h in range(HG):
                    ps, r_ps = work[nw % 4]
                    nw += 1
                    for c in range(4):
                        P.mm(ps[:], wqn[:, c, h, :], cqn[:, c, :], c == 0, c == 3, [r_wqn, r_cqn], [r_ps])
                    P.act(qns[:, h, :], ps[:], AF.Identity, [r_ps], [r_qns])
                    for c in range(4):
                        P.mm(psA[0][0:64, :], wqr[:, c, h, :], cqn[:, c, :], c == 0, c == 3, [r_wqr, r_cqn], [psA[1]])
                    for c in range(4):
                        P.mm(psB[0][0:64, :], wqrot[:, c, h, :], cqn[:, c, :], c == 0, c == 3, [r_wqrot, r_cqn],
                             [psB[1]])
                    a_, r_a = ra[nr % 2]
                    b_, r_b = rb[nr % 2]
                    nr += 1
                    P.tt("dve", a_[:], psA[0][0:64, :], cosT[:, tok], ALU.mult, [psA[1], r_cos], [r_a])
                    P.tt("dve", b_[:], psB[0][0:64, :], sinT[:, tok], ALU.mult, [psB[1], r_sin], [r_b])
                    P.tt("pool", qrs[:, h, :], a_[:], b_[:], ALU.add, [r_a, r_b], [r_qrs])
                P.dma("sp", QN_v[:, :, tok], qns[:], d_qns, reads=[r_qns])
                P.dma("sp", QR_v[:, :, tok], qrs[:], d_qrs, reads=[r_qrs])
                for h in range(HG):
                    ps, r_ps = work[nw % 4]
                    nw += 1
                    for c in range(4):
                        P.mm(ps[:], wk[:, c, h, :], ckvn[:, c, :], c == 0, c == 3, [r_wk, r_ckvn], [r_ps])
                    P.act(kns[:, h, :], ps[:], AF.Identity, [r_ps], [r_kns])
                P.dma("sp", KN_v[:, :, tok], kns[:], d_kns, reads=[r_kns])
                for s in range(4):
                    for half in range(2):
                        ps, r_ps = work[nw % 4]
                        nw += 1
                        for c in range(4):
                            P.mm(ps[:], ckvn[:, c, s * 128:(s + 1) * 128],
                                 wv[:, c, half * 4:(half + 1) * 4, :], c == 0, c == 3, [r_ckvn, r_wv], [r_ps])
                        P.cp("dve", vs[:, s, half * 512:(half + 1) * 512], ps[:], [r_ps], [r_vs])
                P.dma("sp", V_v[:, tt * 4:(tt + 1) * 4, :], vs[:], d_vs, reads=[r_vs])
            ph.end()

        with contextlib.ExitStack() as es:
            ph = Ph(P, es)
            kr, r_kr = ph.sb([64, S], BF16)
            P.dma("sp", kr[:], KR, ph.dsem(), writes=[r_kr])
            slots = []
            for _ in range(2):
                slots.append(dict(qn=ph.sb([128, S], BF16) + (ph.dsem(),), qr=ph.sb([64, S], BF16) + (ph.dsem(),),
                                  kn=ph.sb([128, S], BF16) + (ph.dsem(),), v=ph.sb([128, S // 128, 128], BF16) + (ph.dsem(),)))

            def load_head(h, slot):
                sl = slots[slot]
                qn, r_qn, d_qn = sl["qn"]
                qr_, r_qr, d_qr = sl["qr"]
                kn, r_kn, d_kn = sl["kn"]
                v_, r_v, d_v = sl["v"]
                P.dma("sp", qn[:], QN[h], d_qn, writes=[r_qn])
                P.dma("sp", qr_[:], QR[h], d_qr, writes=[r_qr])
                P.dma("sp", kn[:], KN[h], d_kn, writes=[r_kn])
                P.dma("sp", v_[:], V_v[:, :, h * 128:(h + 1) * 128], d_v, writes=[r_v])
                parts = [
                    (lambda kt, kn=kn: kn[:, kt * 128:(kt + 1) * 128], lambda qt, qn=qn: qn[:, qt * 512:(qt + 1) * 512],
                     [r_kn, r_qn]),
                    (lambda kt: kr[:, kt * 128:(kt + 1) * 128], lambda qt, qr_=qr_: qr_[:, qt * 512:(qt + 1) * 512],
                     [r_kr, r_qr]),
                ]
                return dict(parts=parts, V=(v_, r_v))

            attn_core(P, ph, HG, load_head, scale, maskc, oT_v, kb_d)
            ph.end()
        ph0.end()


def mask_consts():
    kp = np.arange(128)[:, None, None]
    j = np.arange(4)[None, :, None]
    ql = np.arange(512)[None, None, :]
    causal = (ql >= 128 * j + kp)
    chunk = (ql // 64) >= ((128 * j + kp) // 64)
    return (np.ascontiguousarray(chunk.astype(np.float32).astype(ml_dtypes.bfloat16)),
            np.ascontiguousarray(causal.astype(np.float32).astype(ml_dtypes.bfloat16)))


def inv_freq64():
    f = (np.float32(10000.0) ** (-np.arange(0, 64, 2, dtype=np.float32) / np.float32(64))).astype(np.float32)
    return np.ascontiguousarray(np.concatenate([f, f])[:, None])


def fox_part(P, nc, xT, wq_d, wk_d, wv_d, wf_d, bfr, tri, elast, maskc, kb_d, oT, QF, KF, V, qt0=4):
    xT_v = xT.rearrange("(c p) t -> p c t", p=128)
    wq_v = wq_d.rearrange("(c p) n -> p c n", p=128)
    wk_v = wk_d.rearrange("(c p) n -> p c n", p=128)
    wv_v = wv_d.rearrange("(c p) n -> p c n", p=128)
    wf_v = wf_d.rearrange("(c p) n -> p c n", p=128)
    QF_v = QF.rearrange("h p t -> p h t")
    KF_v = KF.rearrange("h p t -> p h t")
    V_v = V.rearrange("(n p) e -> p n e", p=128)
    oT_v = oT.rearrange("(h p) t -> p h t", p=128)
    scale = float(128 ** -0.5)
    NTK = S // 128

    with contextlib.ExitStack() as es0:
        ph0 = Ph(P, es0)
        cneg, r_cneg = ph0.sb([128, NTK, HG], F32)
        cref, r_cref = ph0.sb([128, S // 512, HG], F32)
        with contextlib.ExitStack() as es:
            ph = Ph(P, es)
            wq, r_wq = ph.sb([128, NCH, HG * 128], BF16)
            wk, r_wk = ph.sb([128, NCH, HG * 128], BF16)
            wv, r_wv = ph.sb([128, NCH, HG * 128], BF16)
            wf, r_wf = ph.sb([128, NCH, HG], BF16)
            xb = [ph.sb([128, NCH, 512], BF16) + (ph.dsem(),) for _ in range(2)]
            P.dma("pool", xb[0][0][:], xT_v[:, :, 0:512], xb[0][2], writes=[xb[0][1]])
            for (t_, r_, v_) in ((wk, r_wk, wk_v), (wv, r_wv, wv_v)):
                d_ = ph.dsem()
                for c4 in range(4):
                    P.dma("pool", t_[:, 4 * c4:4 * c4 + 4, :], v_[:, 4 * c4:4 * c4 + 4, :], d_, writes=[r_])
            P.dma("pool", wf[:], wf_v, ph.dsem(), writes=[r_wf])
            d_ = ph.dsem()
            for c4 in range(4):
                P.dma("pool", wq[:, 4 * c4:4 * c4 + 4, :], wq_v[:, 4 * c4:4 * c4 + 4, :], d_, writes=[r_wq])
            bf_sb, r_bf = ph.sb([128, HG], F32)
            P.dma("sp", bf_sb[:], bfr, ph.dsem(), writes=[r_bf])
            tri_sb, r_tri = ph.sb([128, 128], F32)
            el_sb, r_el = ph.sb([128, 128], F32)
            P.dma("sp", tri_sb[:], tri, ph.dsem(), writes=[r_tri])
            P.dma("sp", el_sb[:], elast, ph.dsem(), writes=[r_el])
            fl, r_fl = ph.sb([128, NTK, HG], F32)
            qs, r_qs = ph.sb([128, HG, 512], BF16)
            ks, r_ks = ph.sb([128, HG, 512], BF16)
            vs, r_vs = ph.sb([128, 4, HG * 128], BF16)
            d_qs, d_ks, d_vs = ph.dsem(), ph.dsem(), ph.dsem()
            work = [ph.ps() for _ in range(6)]
            psF = ph.ps()
            nw = 0
            def load_x(tt):
                x_, r_x, d_x = xb[tt % 2]
                P.dma("pool", x_[:], xT_v[:, :, tt * 512:(tt + 1) * 512], d_x, writes=[r_x])

            for tt in range(S // 512):
                tok = slice(tt * 512, (tt + 1) * 512)
                x_, r_x, d_x = xb[tt % 2]
                if tt + 1 < S // 512:
                    load_x(tt + 1)
                for h in range(HG):
                    if tt >= qt0:
                        ps, r_ps = work[nw % 6]
                        nw += 1
                        for c in range(NCH):
                            P.mm(ps[:], wq[:, c, h * 128:(h + 1) * 128], x_[:, c, :], c == 0, c == NCH - 1,
                                 [r_wq, r_x], [r_ps])
                        P.act(qs[:, h, :], ps[:], AF.Identity, [r_ps], [r_qs])
                    ps, r_ps = work[nw % 6]
                    nw += 1
                    for c in range(NCH):
                        P.mm(ps[:], wk[:, c, h * 128:(h + 1) * 128], x_[:, c, :], c == 0, c == NCH - 1,
                             [r_wk, r_x], [r_ps])
                    P.cp("dve", ks[:, h, :], ps[:], [r_ps], [r_ks])
                if tt >= qt0:
                    P.dma("sp", QF_v[:, :, tok], qs[:], d_qs, reads=[r_qs])
                P.dma("sp", KF_v[:, :, tok], ks[:], d_ks, reads=[r_ks])
                for s in range(4):
                    for half in range(2):
                        ps, r_ps = work[nw % 6]
                        nw += 1
                        for c in range(NCH):
                            P.mm(ps[:], x_[:, c, s * 128:(s + 1) * 128], wv[:, c, half * 512:(half + 1) * 512],
                                 c == 0, c == NCH - 1, [r_x, r_wv], [r_ps])
                        if half == 0:
                            P.act(vs[:, s, 0:512], ps[:], AF.Identity, [r_ps], [r_vs])
                        else:
                            P.cp("dve", vs[:, s, 512:1024], ps[:], [r_ps], [r_vs])
                    for c in range(NCH):
                        P.mm(psF[0][:, 0:HG], x_[:, c, s * 128:(s + 1) * 128], wf[:, c, :], c == 0, c == NCH - 1,
                             [r_x, r_wf], [psF[1]])
                    P.tt("dve", fl[:, tt * 4 + s, :], psF[0][:, 0:HG], bf_sb[:], ALU.add, [psF[1], r_bf], [r_fl])
                P.dma("sp", V_v[:, tt * 4:(tt + 1) * 4, :], vs[:], d_vs, reads=[r_vs])
            P.act(fl[:], fl[:], AF.Exp, [r_fl], [r_fl], scale=-1.0)
            P.ts("dve", fl[:], fl[:], 1.0, None, ALU.add, None, [r_fl], [r_fl])
            P.act(fl[:], fl[:], AF.Ln, [r_fl], [r_fl])
            for i in range(NTK):
                P.mm(psF[0][:, 0:HG], tri_sb[:], fl[:, i, :], True, i == 0, [r_tri, r_fl], [psF[1]])
                if i > 0:
                    P.mm(psF[0][:, 0:HG], el_sb[:], cneg[:, i - 1, :], False, True, [r_el, r_cneg], [psF[1]])
                P.cp("dve", cneg[:, i, :], psF[0][:, 0:HG], [psF[1]], [r_cneg])
            for qt in range(S // 512):
                P.mm(psF[0][:, 0:HG], el_sb[:], cneg[:, 4 * qt + 1, :], True, True, [r_el, r_cneg], [psF[1]])
                P.cp("dve", cref[:, qt, :], psF[0][:, 0:HG], [psF[1]], [r_cref])
            ph.end()

        with contextlib.ExitStack() as es:
            ph = Ph(P, es)
            slots = []
            for _ in range(2):
                slots.append(dict(q=ph.sb([128, S], BF16) + (ph.dsem(),), k=ph.sb([128, S], BF16) + (ph.dsem(),),
                                  v=ph.sb([128, NTK, 128], BF16) + (ph.dsem(),)))

            def load_head(h, slot):
                sl = slots[slot]
                q_, r_q, d_q = sl["q"]
                k_, r_k, d_k = sl["k"]
                v_, r_v, d_v = sl["v"]
                P.dma("sp", q_[:, qt0 * 512:S], QF[h][:, qt0 * 512:S], d_q, writes=[r_q])
                P.dma("sp", k_[:], KF[h], d_k, writes=[r_k])
                P.dma("sp", v_[:], V_v[:, :, h * 128:(h + 1) * 128], d_v, writes=[r_v])
                parts = [(lambda kt, k_=k_: k_[:, kt * 128:(kt + 1) * 128],
                          lambda qt, q_=q_: q_[:, qt * 512:(qt + 1) * 512], [r_k, r_q])]
                return dict(parts=parts, V=(v_, r_v))

            attn_core(P, ph, HG, load_head, scale, maskc, oT_v, kb_d, fox=(cneg, r_cneg, cref, r_cref), qt0=qt0)
            ph.end()
        ph0.end()


def tri_consts():
    j = np.arange(128)[:, None]
    t = np.arange(128)[None, :]
    tri = (j <= t).astype(np.float32)
    el = np.zeros((128, 128), np.float32)
    el[127, :] = 1.0
    return np.ascontiguousarray(tri), el


def build_fused():
    nc = bass.Bass("TRN2", target_bir_lowering=False)
    inp = lambda n, sh, dt=F32: nc.dram_tensor(n, sh, dt, kind="ExternalInput").ap()
    xT = inp("xT", [D, S])
    posr = inp("posr", [64, S], I32)
    invf = inp("invf", [64, 1])
    mla_w_in = inp("mla_w_in", [D, 1088])
    qg = inp("qg", [128, 4])
    kvg = inp("kvg", [128, 4])
    mla_w_q = inp("mla_w_q", [512, 16 * 192])
    mla_w_kv = inp("mla_w_kv", [512, 16 * 256])
    mla_w_o = inp("mla_w_o", [D, D])
    fox_w_in = inp("fox_w_in", [D, 6160])
    bfr = inp("bfr", [128, 16])
    fox_w_o = inp("fox_w_o", [D, D])
    w_gu = [inp(f"w_gu{l}", [D, 2 * HID]) for l in range(2)]
    w_down = [inp(f"w_down{l}", [HID, D]) for l in range(2)]
    lnp = [inp(f"lnp{l}", [128, 64]) for l in range(2)]
    tri = inp("tri", [128, 128])
    elast = inp("elast", [128, 128])
    mchunk = inp("mchunk", [128, 4, 512], BF16)
    mcausal = inp("mcausal", [128, 4, 512], BF16)
    kb = inp("kb", [128, 32])
    H2 = S // 2
    yT = nc.dram_tensor("yT", [D, H2], F32, kind="ExternalOutput").ap()
    scr = lambda n, sh, dt=BF16: nc.dram_tensor(n, sh, dt).ap()
    QN, KN = scr("QN_s", [HG, 128, S]), scr("KN_s", [HG, 128, S])
    QR, KR = scr("QR_s", [HG, 64, S]), scr("KR_s", [64, S])
    V = scr("V_s", [S, HG * 128])
    CQ, CKV = scr("CQ_s", [4, 128, S]), scr("CKV_s", [4, 128, S])
    oT = scr("oT_s", [D, S])
    x1T = scr("x1T_s", [D, S], F32)
    xm = scr("xm_s", [D, S], F32)
    zs = scr("zs_s", [D, S], F32)
    W = HG * 128
    with contextlib.ExitStack() as es:
        P = Prog(nc, es)
        with contextlib.ExitStack() as es_t:
            pht = Ph(P, es_t)
            tabs = rope_tables(P, pht, posr, invf)
            for hg in range(2):
                mla_part(P, nc, xT, tabs, mla_w_in, qg, kvg,
                         mla_w_q[:, hg * HG * 192:(hg + 1) * HG * 192],
                         mla_w_kv[:, hg * HG * 256:(hg + 1) * HG * 256],
                         mchunk, kb, oT[hg * W:(hg + 1) * W, :], QN, QR, KN, KR, V, CQ, CKV, hg == 0)
            pht.end()
        dense_part(P, nc, oT, xT, mla_w_o, lnp[0], w_gu[0], w_down[0], x1T, xm, zs, nt=S, out_is_output=False)
        for hg in range(2):
            fox_part(P, nc, x1T, fox_w_in[:, hg * W:(hg + 1) * W], fox_w_in[:, 2048 + hg * W:2048 + (hg + 1) * W],
                     fox_w_in[:, 4096 + hg * W:4096 + (hg + 1) * W], fox_w_in[:, 6144 + hg * HG:6144 + (hg + 1) * HG],
                     bfr[:, hg * HG:(hg + 1) * HG], tri, elast, mcausal, kb, oT[hg * W:(hg + 1) * W, :], QN, KN, V)
        dense_part(P, nc, oT[:, H2:S], x1T[:, H2:S], fox_w_o, lnp[1], w_gu[1], w_down[1], yT, xm[:, 0:H2],
                   zs[:, 0:H2], nt=H2, out_is_output=True)
        P.finish()
    return nc


def kernel(x, positions, mla_w_in, mla_q_norm_g, mla_w_q_up, mla_kv_norm_g, mla_w_kv_up, mla_w_o,
           fox_w_in, fox_b_f, fox_w_o, ffn_w_gu, ffn_w_down, ln_mix_g, ln_mix_b, ln_ffn_g, ln_ffn_b):
    f = lambda a: np.ascontiguousarray(np.asarray(a, np.float32))
    x = f(x)
    positions = np.asarray(positions, np.int32)
    if "fused" not in _CACHE:
        _CACHE["fused"] = build_fused()
    nc = _CACHE["fused"]
    mchunk, mcausal = mask_consts()
    tri, el = tri_consts()
    common = {
        "invf": inv_freq64(), "mla_w_in": f(mla_w_in)[0],
        "qg": np.ascontiguousarray(f(mla_q_norm_g)[0].reshape(4, 128).T),
        "kvg": np.ascontiguousarray(f(mla_kv_norm_g)[0].reshape(4, 128).T),
        "mla_w_q": f(mla_w_q_up)[0], "mla_w_kv": f(mla_w_kv_up)[0], "mla_w_o": f(mla_w_o)[0],
        "fox_w_in": f(fox_w_in)[0],
        "bfr": np.ascontiguousarray(np.broadcast_to(f(fox_b_f)[0][None, :], (128, 16))),
        "fox_w_o": f(fox_w_o)[0],
        "w_gu0": f(ffn_w_gu)[0], "w_gu1": f(ffn_w_gu)[1], "w_down0": f(ffn_w_down)[0], "w_down1": f(ffn_w_down)[1],
        "lnp0": ln_params(f(ln_mix_g)[0], f(ln_mix_b)[0], f(ln_ffn_g)[0], f(ln_ffn_b)[0]),
        "lnp1": ln_params(f(ln_mix_g)[1], f(ln_mix_b)[1], f(ln_ffn_g)[1], f(ln_ffn_b)[1]),
        "tri": tri, "elast": el, "mchunk": mchunk, "mcausal": mcausal,
    }
    H2 = S // 2
    kb_nat = np.zeros((128, 32), np.float32)
    kb_rot = np.zeros((128, 32), np.float32)
    kb_rot[:, 0:16] = -30000.0
    in_maps = []
    for c in range(8):
        b, rot = c // 2, c % 2
        m = dict(common)
        xb_ = x[b]
        pb_ = positions[b]
        if rot:
            xb_ = np.concatenate([xb_[H2:], xb_[:H2]], axis=0)
            pb_ = np.concatenate([pb_[H2:], pb_[:H2]], axis=0)
        m["xT"] = np.ascontiguousarray(xb_.T)
        m["posr"] = np.ascontiguousarray(np.broadcast_to(pb_[None, :], (64, S)))
        m["kb"] = kb_rot if rot else kb_nat
        in_maps.append(m)
    res = run_bass_kernel_spmd(nc, in_maps, core_ids=list(range(8)))
    out = np.empty((B, S, D), np.float32)
    for c in range(8):
        b, rot = c // 2, c % 2
        if rot:
            out[b, :H2] = res.results[c]["yT"].T
        else:
            out[b, H2:] = res.results[c]["yT"].T
    return out
```

```python
import contextlib
import math
import numpy as np
import ml_dtypes
import concourse.bass as bass
import concourse.mybir as mybir
from concourse.bass_utils import run_bass_kernel_spmd

F32 = mybir.dt.float32
BF16 = mybir.dt.bfloat16
I32 = mybir.dt.int32
AF = mybir.ActivationFunctionType
ALU = mybir.AluOpType

D = 2048
NCH = 16
S = 4096
B = 4
HID = 5632
NJ = 44
ALPHA = float(4 ** 0.25)
LN_EPS = 1e-5
RMS_EPS = 1e-6
NT = 2048
HG = 8
SEM_EPOCH = 30000


class Reg:
    __slots__ = ("name", "lw", "rd", "excl")

    def __init__(self, name):
        self.name = name
        self.lw = None
        self.rd = {}
        self.excl = False


class HSem:
    __slots__ = ("sem", "total", "kind")

    def __init__(self, sem, kind):
        self.sem = sem
        self.total = 0
        self.kind = kind


class DSem:
    __slots__ = ("h",)

    def __init__(self):
        self.h = None


class Prog:
    ENG = ("pe", "act", "dve", "pool", "sp")

    def __init__(self, nc, es, n_dma_sems=76):
        self.nc = nc
        self.q = {e: [] for e in self.ENG}
        self.nsig = {e: 0 for e in self.ENG}
        nes = {"pe": 6, "act": 4, "dve": 4, "pool": 3, "sp": 1}
        self.esems = {e: [es.enter_context(nc.semaphore(f"s_{e}{i}")) for i in range(n)]
                      for e, n in nes.items()}
        n_sw = 26
        self.dpool = [HSem(es.enter_context(nc.semaphore(f"s_d{i}")), "sw" if i < n_sw else "hw")
                      for i in range(n_dma_sems)]
        self.dfree = {"sw": [h for h in self.dpool if h.kind == "sw"],
                      "hw": [h for h in self.dpool if h.kind == "hw"]}
        self.seen = {e: {} for e in self.ENG}
        self.regs = []
        self.out_events = []

    def reg(self, name=""):
        r = Reg(name)
        self.regs.append(r)
        return r

    def dsem(self):
        return DSem()

    def _bind(self, ds, eng):
        kind = "sw" if eng == "pool" else "hw"
        if ds.h is None:
            ds.h = self.dfree[kind].pop()
        assert ds.h.kind == kind, "a DMA semaphore must stay on one kind of DMA queue"
        return ds.h

    def _deps(self, eng, reads, writes):
        deps = []
        for r in reads:
            if r.lw is not None:
                deps.append(r.lw)
        for w in writes:
            if w.lw is not None:
                deps.append(w.lw)
            deps.extend(w.rd.values())
        out = []
        seen = set()
        for d in deps:
            if d[0] == "c" and d[1] == "pe" and eng == "pe":
                continue
            k = (d[0], id(d[1]) if d[0] == "d" else d[1], d[2])
            if k in seen:
                continue
            seen.add(k)
            out.append(d)
        for d in out:
            if d[0] == "c":
                self.q[d[1]][d[2]]["sig"] = True
        return out

    def _commit(self, ev, reads, writes, rkey):
        for r in reads:
            r.rd[rkey] = ev
        for w in writes:
            w.lw = ev
            w.rd = {}

    def op(self, eng, fn, reads=(), writes=()):
        ex = [r for r in reads if r.excl]
        if ex:
            writes = list(writes) + ex
            reads = [r for r in reads if not r.excl]
        idx = len(self.q[eng])
        deps = self._deps(eng, reads, writes)
        self.q[eng].append({"fn": fn, "deps": deps, "sig": False, "dma": None})
        ev = ("c", eng, idx)
        self._commit(ev, reads, writes, eng)
        return ev

    def dma(self, eng, out, in_, ds, reads=(), writes=(), is_output=False):
        deps = self._deps(eng, reads, writes)
        ds = self._bind(ds, eng)
        ds.total += 16
        ev = ("d", ds, ds.total)
        self.q[eng].append({"fn": None, "deps": deps, "sig": False, "dma": (out, in_, ds)})
        self._commit(ev, reads, writes, ("d", id(ds), ds.total))
        if is_output:
            self.out_events.append(ev)
        return ev

    def dmafn(self, eng, fn, ds, reads=(), writes=(), inc=1):
        deps = self._deps(eng, reads, writes)
        ds = self._bind(ds, eng)
        ds.total += inc
        ev = ("d", ds, ds.total)
        self.q[eng].append({"fn": None, "deps": deps, "sig": False, "dma": (fn, inc, ds)})
        self._commit(ev, reads, writes, ("d", id(ds), ds.total))
        return ev

    def barrier(self):
        evs = []
        for e in self.ENG:
            idx = len(self.q[e]) - 1
            while idx >= 0 and self.q[e][idx]["fn"] is None:
                idx -= 1
            if idx >= 0:
                self.q[e][idx]["sig"] = True
                evs.append(("c", e, idx))
        for ds in self.dpool:
            if ds.total:
                evs.append(("d", ds, ds.total))
        for e in self.ENG:
            self.q[e].append({"fn": None, "deps": [d for d in evs if not (d[0] == "c" and d[1] == e)],
                              "sig": False, "dma": None})
        for r in self.regs:
            r.lw = None
            r.rd = {}

    def emit(self):
        nc = self.nc
        sigval = {}
        for e in self.ENG:
            n = self.nsig[e]
            for idx, ins in enumerate(self.q[e]):
                if ins["sig"]:
                    n += 1
                    sigval[(e, idx)] = n
            self.nsig[e] = n
            assert n < SEM_EPOCH * len(self.esems[e]), (e, n)

        def sem_of(e, n):
            k = (n - 1) // SEM_EPOCH
            return self.esems[e][k], n - k * SEM_EPOCH

        def run(e, engobj):
            seen = self.seen[e]
            for idx, ins in enumerate(self.q[e]):
                for d in ins["deps"]:
                    if d[0] == "c":
                        sem, val = sem_of(d[1], sigval[(d[1], d[2])])
                    else:
                        sem, val = d[1].sem, d[2]
                    key = id(sem)
                    if seen.get(key, 0) >= val:
                        continue
                    seen[key] = val
                    engobj.wait_ge(sem, val)
                if ins["dma"] is not None:
                    out, in_, ds = ins["dma"]
                    if callable(out):
                        out(engobj).then_inc(ds.sem, in_)
                    else:
                        engobj.dma_start(out=out, in_=in_).then_inc(ds.sem, 16)
                elif ins["fn"] is not None:
                    bi = ins["fn"](engobj)
                    if ins["sig"]:
                        sem, _ = sem_of(e, sigval[(e, idx)])
                        bi.then_inc(sem, 1)

        with nc.Block() as block:
            @block.tensor
            def _(eng):
                run("pe", eng)

            @block.scalar
            def _(eng):
                run("act", eng)

            @block.vector
            def _(eng):
                run("dve", eng)

            @block.gpsimd
            def _(eng):
                run("pool", eng)

            @block.sync
            def _(eng):
                run("sp", eng)
        self.q = {e: [] for e in self.ENG}
        for r in self.regs:
            r.lw = None
            r.rd = {}

    def phase_end(self, dsems=()):
        self.barrier()
        self.emit()
        for ds in dsems:
            if ds.h is not None:
                self.dfree[ds.h.kind].append(ds.h)
                ds.h = None

    def finish(self):
        self.q["sp"].append({"fn": None, "deps": list(self.out_events), "sig": False, "dma": None})
        self.barrier()
        self.emit()

    def mm(self, out, lhsT, rhs, start, stop, reads, writes):
        return self.op("pe", lambda e: e.matmul(out, lhsT, rhs, start=start, stop=stop), reads, writes)

    def act(self, out, in_, func, reads, writes, bias=None, scale=1.0):
        if bias is None:
            return self.op("act", lambda e: e.activation(out=out, in_=in_, func=func, scale=scale), reads, writes)
        return self.op("act", lambda e: e.activation(out=out, in_=in_, func=func, bias=bias, scale=scale),
                       reads, writes)

    def tt(self, eng, out, in0, in1, op, reads, writes):
        return self.op(eng, lambda e: e.tensor_tensor(out, in0, in1, op), reads, writes)

    def ts(self, eng, out, in0, s1, s2, op0, op1, reads, writes):
        if s2 is None:
            return self.op(eng, lambda e: e.tensor_scalar(out, in0, s1, None, op0), reads, writes)
        return self.op(eng, lambda e: e.tensor_scalar(out, in0, s1, s2, op0, op1), reads, writes)

    def stt(self, eng, out, in0, scalar, in1, op0, op1, reads, writes):
        return self.op(eng, lambda e: e.scalar_tensor_tensor(out, in0, scalar, in1, op0, op1), reads, writes)

    def cp(self, eng, out, in_, reads, writes):
        return self.op(eng, lambda e: e.tensor_copy(out, in_), reads, writes)

    def memset(self, eng, ap, val, writes):
        return self.op(eng, lambda e: e.memset(ap, val), (), writes)


class Ph:
    CNT = 0

    def __init__(self, P, es):
        self.P = P
        self.es = es
        self.ds = []
        self.n = 0

    def sb(self, shape, dt, nreg=0):
        Ph.CNT += 1
        t = self.es.enter_context(self.P.nc.sbuf_tensor(f"sb_{Ph.CNT}", list(shape), dt))
        t_r = self.P.reg() if nreg == 0 else [self.P.reg() for _ in range(nreg)]
        return t, t_r

    def ps(self, shape=(128, 512), dt=F32):
        Ph.CNT += 1
        t = self.es.enter_context(self.P.nc.psum_tensor(f"ps_{Ph.CNT}", list(shape), dt))
        r = self.P.reg()
        r.excl = True
        return t, r

    def dsem(self):
        d = self.P.dsem()
        self.ds.append(d)
        return d

    def end(self):
        self.P.phase_end(self.ds)
        self.ds = []


def ln_stats(P, psS1, psS2, mean_sb, tmp, rstd, nmr, eps):
    (s1, r_s1), (s2, r_s2) = psS1, psS2
    (m, r_m), (t, r_t), (rs, r_rs), (nm, r_nm) = mean_sb, tmp, rstd, nmr
    P.cp("dve", m[:], s1[:], [r_s1], [r_m])
    P.tt("dve", t[:], m[:], m[:], ALU.mult, [r_m], [r_t])
    P.tt("dve", t[:], s2[:], t[:], ALU.subtract, [r_s2, r_t], [r_t])
    P.ts("dve", t[:], t[:], eps, None, ALU.add, None, [r_t], [r_t])
    P.act(rs[:], t[:], AF.Sqrt, [r_t], [r_rs])
    P.op("dve", lambda e: e.reciprocal(rs[:], rs[:]), [r_rs], [r_rs])
    P.stt("dve", nm[:], m[:], -1.0, rs[:], ALU.mult, ALU.mult, [r_m, r_rs], [r_nm])


def ln_apply(P, zc, r_z, rstd, nmr, t1, t2, g_ap, b_ap, r_par, out, r_out, mul_eng="dve"):
    (rs, r_rs), (nm, r_nm), (a, r_a), (b_, r_b) = rstd, nmr, t1, t2
    P.tt(mul_eng, a[:], zc, rs[:], ALU.mult, [r_z, r_rs], [r_a])
    P.tt("pool", b_[:], a[:], nm[:], ALU.add, [r_a, r_nm], [r_b])
    P.act(out, b_[:], AF.Identity, [r_b, r_par], [r_out], bias=b_ap, scale=g_ap)


def dense_part(P, nc, oT, xT, w_o, lnp, w_gu, w_down, yT, xm, zs, nt=NT, out_is_output=True, TT=1024):
    oT_v = oT.rearrange("(c p) t -> p c t", p=128)
    xT_v = xT.rearrange("(c p) t -> p c t", p=128)
    yT_v = yT.rearrange("(c p) t -> p c t", p=128)
    xm_v = xm.rearrange("(c p) t -> p c t", p=128)
    zs_v = zs.rearrange("(c p) t -> p c t", p=128)
    wo_v = w_o.rearrange("(c p) n -> p c n", p=128)
    wgu_v = w_gu.rearrange("(c p) n -> p c n", p=128)
    wd_v = w_down.rearrange("(j p) n -> p j n", p=128)

    with contextlib.ExitStack() as es:
        ph = Ph(P, es)
        wo_sb, r_wo4 = ph.sb([128, NCH, D], BF16, nreg=4)
        for cb in range(4):
            P.dma("pool", wo_sb[:, :, cb * 512:(cb + 1) * 512], wo_v[:, :, cb * 512:(cb + 1) * 512], ph.dsem(),
                  writes=[r_wo4[cb]])
        lnp_sb, r_ln = ph.sb([128, 64], F32)
        P.dma("sp", lnp_sb[:], lnp, ph.dsem(), writes=[r_ln])
        onesD, r_1 = ph.sb([128, 128], BF16)
        P.memset("pool", onesD[:], 1.0 / D, [r_1])
        oTt = [ph.sb([128, NCH, 512], BF16) + (ph.dsem(),) for _ in range(2)]
        xr = [ph.sb([128, 512], F32) + (ph.dsem(),) for _ in range(4)]
        zz = [ph.sb([128, NCH, 512], F32, nreg=NCH) for _ in range(2)]
        zb = [ph.sb([128, 512], BF16) for _ in range(3)]
        sq = [ph.sb([128, 512], BF16) for _ in range(3)]
        psY = [ph.ps() for _ in range(2)]
        psS1, psS2 = ph.ps(), ph.ps()
        mean_sb, tmp, rstd, nmr = [ph.sb([128, 512], F32) for _ in range(4)]
        t1 = [ph.sb([128, 512], F32) for _ in range(2)]
        t2 = [ph.sb([128, 512], F32) for _ in range(2)]
        yo = [ph.sb([128, 512], F32) + (ph.dsem(),) for _ in range(3)]
        n = 0
        def load_o(tt):
            o, r_o, d_o = oTt[tt % 2]
            P.dma("sp", o[:], oT_v[:, :, tt * 512:(tt + 1) * 512], d_o, writes=[r_o])

        load_o(0)

        def norm_gen(tt, z, r_z):
            tok = slice(tt * 512, (tt + 1) * 512)
            for oc in range(NCH):
                k = tt * NCH + oc
                y_, r_y, d_y = yo[k % 3]
                ln_apply(P, z[:, oc, :], r_z[oc], rstd, nmr, t1[k % 2], t2[k % 2],
                         lnp_sb[:, oc:oc + 1], lnp_sb[:, 16 + oc:17 + oc], r_ln, y_[:], r_y, mul_eng="pool")
                P.dma("act", xm_v[:, oc, tok], y_[:], d_y, reads=[r_y])
                yield

        gen = iter(())
        for tt in range(nt // 512):
            tok = slice(tt * 512, (tt + 1) * 512)
            o, r_o, d_o = oTt[tt % 2]
            z, r_z = zz[tt % 2]
            if tt + 1 < nt // 512:
                load_o(tt + 1)
            pend = None
            for oc in range(NCH):
                ps, r_ps = psY[oc % 2]
                for c in range(NCH):
                    P.mm(ps[:], wo_sb[:, c, oc * 128:(oc + 1) * 128], o[:, c, :], c == 0, c == NCH - 1,
                         [r_wo4[oc // 4], r_o], [r_ps])
                if pend is not None:
                    pend()
                x_, r_x, d_x = xr[n % 4]
                P.dma("sp", x_[:], xT_v[:, oc, tok], d_x, writes=[r_x])
                P.stt("dve", z[:, oc, :], x_[:], ALPHA, ps[:], ALU.mult, ALU.add, [r_x, r_ps], [r_z[oc]])
                zb_, r_zb = zb[n % 3]
                sq_, r_sq = sq[n % 3]
                P.cp("dve", zb_[:], z[:, oc, :], [r_z[oc]], [r_zb])
                P.act(sq_[:], z[:, oc, :], AF.Square, [r_z[oc]], [r_sq])

                def pend(oc=oc, zb_=zb_, r_zb=r_zb, sq_=sq_, r_sq=r_sq):
                    P.mm(psS1[0][:], onesD[:], zb_[:], oc == 0, oc == NCH - 1, [r_1, r_zb], [psS1[1]])
                    P.mm(psS2[0][:], onesD[:], sq_[:], oc == 0, oc == NCH - 1, [r_1, r_sq], [psS2[1]])
                n += 1
                next(gen, None)
            pend()
            for _ in gen:
                pass
            ln_stats(P, psS1, psS2, mean_sb, tmp, rstd, nmr, LN_EPS)
            gen = norm_gen(tt, z, r_z)
        for _ in gen:
            pass
        ph.end()

    NTT = TT // 512
    with contextlib.ExitStack() as es0:
        ph0 = Ph(P, es0)
        hT, r_h = ph0.sb([128, NJ, TT], BF16, nreg=NJ * NTT)
        psS1 = [ph0.ps() for _ in range(NTT)]
        psS2 = [ph0.ps() for _ in range(NTT)]
        rstd = [ph0.sb([128, 512], F32) for _ in range(NTT)]
        nmr = [ph0.sb([128, 512], F32) for _ in range(NTT)]
        lnp_sb, r_ln = ph0.sb([128, 64], F32)
        onesD, r_1 = ph0.sb([128, 128], BF16)
        xb, r_xb = ph0.sb([128, NCH, TT], BF16)
        d_xb = ph0.dsem()
        wdA = ph0.sb([128, NJ, 256], BF16) + (ph0.dsem(),)

        def load_xb(T0x):
            for c4 in range(4):
                P.dma("pool", xb[:, 4 * c4:4 * c4 + 4, :], xm_v[:, 4 * c4:4 * c4 + 4, T0x:T0x + TT], d_xb,
                      writes=[r_xb])

        def load_wd_to(slot, oc2):
            w_, r_w, d_w = slot
            P.dma("pool", w_[:, 0:NJ // 2, :], wd_v[:, 0:NJ // 2, oc2 * 256:(oc2 + 1) * 256], d_w, writes=[r_w])
            P.dma("pool", w_[:, NJ // 2:NJ, :], wd_v[:, NJ // 2:NJ, oc2 * 256:(oc2 + 1) * 256], d_w, writes=[r_w])

        first = True
        prevT0 = None

        def f_pass(ph, T0p):
            zin = [ph.sb([128, 512], F32) + (ph.dsem(),) for _ in range(3)]
            t1 = [ph.sb([128, 512], F32) for _ in range(2)]
            t2 = [ph.sb([128, 512], F32) for _ in range(2)]
            yo = [ph.sb([128, 512], F32) + (ph.dsem(),) for _ in range(3)]
            n = 0
            for oc in range(NCH):
                for tt in range(NTT):
                    gtok = slice(T0p + tt * 512, T0p + (tt + 1) * 512)
                    zi, r_zi, d_zi = zin[n % 3]
                    P.dma("sp", zi[:], zs_v[:, oc, gtok], d_zi, writes=[r_zi])
                    y_, r_y, d_y = yo[n % 3]
                    ln_apply(P, zi[:], r_zi, rstd[tt], nmr[tt], t1[n % 2], t2[n % 2],
                             lnp_sb[:, 32 + oc:33 + oc], lnp_sb[:, 48 + oc:49 + oc], r_ln, y_[:], r_y)
                    P.dma("act", yT_v[:, oc, gtok], y_[:], d_y, reads=[r_y], is_output=out_is_output)
                    n += 1
                    yield

        for T0 in range(0, nt, TT):
            with contextlib.ExitStack() as es:
                ph = Ph(P, es)
                if first:
                    P.dma("sp", lnp_sb[:], lnp, ph.dsem(), writes=[r_ln])
                    P.memset("pool", onesD[:], 1.0 / D, [r_1])
                    first = False
                    load_xb(T0)
                wg = [ph.sb([128, NCH, 256], BF16) + (ph.dsem(),) for _ in range(2)]
                wu = [ph.sb([128, NCH, 256], BF16) + (ph.dsem(),) for _ in range(2)]
                sg = [ph.sb([128, 512], F32) for _ in range(2)]
                psG = [ph.ps() for _ in range(2)]
                psU = [ph.ps() for _ in range(2)]
                n = 0
                fgen = f_pass(ph, prevT0) if prevT0 is not None else iter(())
                def load_gu(j2):
                    g_, r_g, d_g = wg[j2 % 2]
                    u_, r_u, d_u = wu[j2 % 2]
                    P.dma("pool", g_[:], wgu_v[:, :, j2 * 256:(j2 + 1) * 256], d_g, writes=[r_g])
                    P.dma("pool", u_[:], wgu_v[:, :, HID + j2 * 256:HID + (j2 + 1) * 256], d_u, writes=[r_u])

                load_gu(0)
                load_gu(1)
                load_wd_to(wdA, 0)
                for j2 in range(NJ // 2):
                    g_, r_g, d_g = wg[j2 % 2]
                    u_, r_u, d_u = wu[j2 % 2]
                    if 1 <= j2 and j2 + 1 < NJ // 2:
                        load_gu(j2 + 1)
                    for _ in range(2):
                        next(fgen, None)
                    for jj in range(2):
                        j = 2 * j2 + jj
                        for tt in range(NTT):
                            tok = slice(tt * 512, (tt + 1) * 512)
                            pg, r_pg = psG[n % 2]
                            pu, r_pu = psU[n % 2]
                            s_, r_s = sg[n % 2]
                            for c in range(NCH):
                                P.mm(pg[:], g_[:, c, jj * 128:(jj + 1) * 128], xb[:, c, tok], c == 0, c == NCH - 1,
                                     [r_g, r_xb], [r_pg])
                            for c in range(NCH):
                                P.mm(pu[:], u_[:, c, jj * 128:(jj + 1) * 128], xb[:, c, tok], c == 0, c == NCH - 1,
                                     [r_u, r_xb], [r_pu])
                            P.act(s_[:], pg[:], AF.Silu, [r_pg], [r_s])
                            P.tt("dve", hT[:, j, tok], s_[:], pu[:], ALU.mult, [r_s, r_pu], [r_h[j * NTT + tt]])
                            n += 1
                for _ in fgen:
                    pass
                ph.end()
            with contextlib.ExitStack() as es:
                ph = Ph(P, es)
                wd = [wdA, ph.sb([128, NJ, 256], BF16) + (ph.dsem(),)]
                xres = [ph.sb([128, 512], F32) + (ph.dsem(),) for _ in range(2)]
                zt = [ph.sb([128, 512], F32) + (ph.dsem(),) for _ in range(3)]
                zb = [ph.sb([128, 512], BF16) for _ in range(2)]
                sq = [ph.sb([128, 512], BF16) for _ in range(2)]
                psY = [ph.ps() for _ in range(2)]
                mean_sb, tmp = [ph.sb([128, 512], F32) for _ in range(2)]
                n = 0
                pend = None
                zb = zb + [ph.sb([128, 512], BF16)]
                sq = sq + [ph.sb([128, 512], BF16)]
                for oc2 in range(NCH // 2):
                    w_, r_w, d_w = wd[oc2 % 2]
                    if oc2 + 1 < NCH // 2:
                        load_wd_to(wd[(oc2 + 1) % 2], oc2 + 1)
                    if oc2 == 0 and T0 + TT < nt:
                        load_xb(T0 + TT)
                    for ocs in range(2):
                        oc = 2 * oc2 + ocs
                        for tt in range(NTT):
                            tok = slice(tt * 512, (tt + 1) * 512)
                            gtok = slice(T0 + tt * 512, T0 + (tt + 1) * 512)
                            ps, r_ps = psY[n % 2]
                            for j in range(NJ):
                                P.mm(ps[:], w_[:, j, ocs * 128:(ocs + 1) * 128], hT[:, j, tok], j == 0, j == NJ - 1,
                                     [r_w, r_h[j * NTT + tt]], [r_ps])
                            if pend is not None:
                                pend()
                            x_, r_x, d_x = xres[n % 2]
                            P.dma("sp", x_[:], xm_v[:, oc, gtok], d_x, writes=[r_x])
                            z_, r_zt, d_z = zt[n % 3]
                            P.stt("dve", z_[:], x_[:], ALPHA, ps[:], ALU.mult, ALU.add, [r_x, r_ps], [r_zt])
                            P.dma("act", zs_v[:, oc, gtok], z_[:], d_z, reads=[r_zt])
                            zb_, r_zb = zb[n % 3]
                            sq_, r_sq = sq[n % 3]
                            P.cp("dve", zb_[:], z_[:], [r_zt], [r_zb])
                            P.act(sq_[:], z_[:], AF.Square, [r_zt], [r_sq])

                            def pend(oc=oc, tt=tt, zb_=zb_, r_zb=r_zb, sq_=sq_, r_sq=r_sq):
                                P.mm(psS1[tt][0][:], onesD[:], zb_[:], oc == 0, oc == NCH - 1, [r_1, r_zb],
                                     [psS1[tt][1]])
                                P.mm(psS2[tt][0][:], onesD[:], sq_[:], oc == 0, oc == NCH - 1, [r_1, r_sq],
                                     [psS2[tt][1]])
                            n += 1
                pend()
                for tt in range(NTT):
                    ln_stats(P, psS1[tt], psS2[tt], mean_sb, tmp, rstd[tt], nmr[tt], LN_EPS)
                ph.end()
            prevT0 = T0
        with contextlib.ExitStack() as es:
            ph = Ph(P, es)
            for _ in f_pass(ph, prevT0):
                pass
            ph.end()
        ph0.end()


def chunk_layout(v):
    return np.ascontiguousarray(v.reshape(-1, 128).T)


def ln_params(g1, b1, g2, b2):
    return np.ascontiguousarray(np.concatenate(
        [chunk_layout(g1), chunk_layout(b1), chunk_layout(g2), chunk_layout(b2)], axis=1).astype(np.float32))


_CACHE = {}


def attn_core(P, ph, nheads, load_head, scale, maskc, oT_v, kb_d, fox=None, qt0=0):
    NQT = S // 512
    LA = 3
    NPT = 7
    pT = [ph.sb([128, 512], BF16) for _ in range(NPT)]
    psS = [ph.ps() for _ in range(4)]
    psO = [ph.ps() for _ in range(2)]
    psR = [ph.ps() for _ in range(2)]
    ones, r_ones = ph.sb([128, 128], BF16)
    P.memset("pool", ones[:], 1.0, [r_ones])
    mk, r_mk = ph.sb([128, 4, 512], BF16)
    P.dma("sp", mk[:], maskc, ph.dsem(), writes=[r_mk])
    rinv = [ph.sb([128, 512], F32) for _ in range(2)]
    ost = [ph.sb([128, S], BF16) + (ph.dsem(),) for _ in range(2)]
    kb, r_kb = ph.sb([128, 32], F32)
    P.dma("sp", kb[:], kb_d, ph.dsem(), writes=[r_kb])
    if fox is not None:
        cneg, r_cneg, cref, r_cref = fox
        bia = [ph.sb([128, 32], F32) for _ in range(2)]
    tiles = []
    nq = 0
    for h in range(nheads):
        for qt in range(qt0, NQT):
            nk = 4 * (qt + 1)
            for kt in range(nk):
                tiles.append((h, qt, kt, nk, nq))
            nq += 1
    heads = {0: load_head(0, 0)}

    def issue_S(i):
        h, qt, kt, nk, nq = tiles[i]
        hd = heads[h]
        if fox is not None and kt == 0:
            b_, r_b = bia[nq % 2]
            P.ts("dve", b_[:, 0:nk], cneg[:, 0:nk, h], cref[:, qt, h:h + 1], None, ALU.subtract, None,
                 [r_cneg, r_cref], [r_b])
            if qt >= NQT // 2:
                P.tt("dve", b_[:, 0:nk], b_[:, 0:nk], kb[:, 0:nk], ALU.add, [r_b, r_kb], [r_b])
        ps, r_ps = psS[i % 4]
        parts = hd["parts"]
        j = kt - 4 * qt
        lo = 128 * j if j > 0 else 0
        for k, (kf, qf, regs) in enumerate(parts):
            P.mm(ps[:, lo:512], kf(kt), qf(qt)[:, lo:512], k == 0, k == len(parts) - 1, regs, [r_ps])
        p_, r_p = pT[i % NPT]
        if fox is not None:
            b_, r_b = bia[nq % 2]
            P.act(p_[:, lo:512], ps[:, lo:512], AF.Exp, [r_ps, r_b], [r_p], bias=b_[:, kt:kt + 1], scale=scale)
        elif qt >= NQT // 2:
            P.act(p_[:, lo:512], ps[:, lo:512], AF.Exp, [r_ps, r_kb], [r_p], bias=kb[:, kt:kt + 1], scale=scale)
        else:
            P.act(p_[:, lo:512], ps[:, lo:512], AF.Exp, [r_ps], [r_p], scale=scale)
        if j >= 0:
            P.tt("pool", p_[:, lo:lo + 128], p_[:, lo:lo + 128], mk[:, j, lo:lo + 128], ALU.mult, [r_p, r_mk], [r_p])

    def issue_PV(i):
        h, qt, kt, nk, nq = tiles[i]
        if qt == qt0 and kt == 0 and h + 1 < nheads:
            heads[h + 1] = load_head(h + 1, (h + 1) % 2)
        hd = heads[h]
        p_, r_p = pT[i % NPT]
        po, r_po = psO[nq % 2]
        vt, r_v = hd["V"]
        j = kt - 4 * qt
        lo = 128 * j if j > 0 else 0
        P.op("pe", lambda e, po=po, vt=vt, p_=p_, kt=kt, nk=nk, lo=lo: e.matmul(
            po[:, lo:512], vt[:, kt, :], p_[:, lo:512], start=(kt == 0), stop=(kt == nk - 1),
            skip_group_check=True), [r_v, r_p], [r_po])
        pr, r_pr = psR[nq % 2]
        P.op("pe", lambda e, pr=pr, p_=p_, kt=kt, nk=nk, lo=lo: e.matmul(
            pr[:, lo:512], ones[:], p_[:, lo:512], start=(kt == 0), stop=(kt == nk - 1),
            skip_group_check=True), [r_ones, r_p], [r_pr])
        if kt == nk - 1:
            ri, r_ri = rinv[nq % 2]
            o_, r_o, d_o = ost[h % 2]
            P.op("dve", lambda e, ri=ri, pr=pr: e.reciprocal(ri[:], pr[:]), [r_pr], [r_ri])
            P.tt("dve", o_[:, qt * 512:(qt + 1) * 512], po[:], ri[:], ALU.mult, [r_po, r_ri], [r_o])
            if qt == NQT - 1:
                P.dma("sp", oT_v[:, h, qt0 * 512:S], o_[:, qt0 * 512:S], d_o, reads=[r_o])

    N = len(tiles)
    for i in range(N + LA):
        if i < N:
            issue_S(i)
        if i - LA >= 0:
            issue_PV(i - LA)


TWO_PI = 2.0 * math.pi
CW1 = 6.28125
CW2 = float(np.float32(TWO_PI - 6.28125))
CW3 = float(TWO_PI - 6.28125 - float(np.float32(TWO_PI - 6.28125)))
MAGIC = 12582912.0
PI_SAFE = 3.1415925


def rope_tables(P, pht, posr, invf):
    cosT, r_cos = pht.sb([64, S], F32)
    sinT, r_sin = pht.sb([64, S], F32)
    with contextlib.ExitStack() as es:
        ph = Ph(P, es)
        posi, r_pi = ph.sb([64, S], I32)
        P.dma("sp", posi[:], posr, ph.dsem(), writes=[r_pi])
        ivf, r_iv = ph.sb([64, 1], F32)
        P.dma("sp", ivf[:], invf, ph.dsem(), writes=[r_iv])
        CB = 1024
        ang, r_a = ph.sb([64, CB], F32)
        kk, r_k = ph.sb([64, CB], F32)
        rr, r_r = ph.sb([64, CB], F32)
        yy, r_y = ph.sb([64, CB], F32)
        mm_, r_m = ph.sb([64, CB], F32)
        for cb in range(S // CB):
            sl = slice(cb * CB, (cb + 1) * CB)
            P.cp("dve", ang[:], posi[:, sl], [r_pi], [r_a])
            P.ts("dve", ang[:], ang[:], ivf[:, 0:1], None, ALU.mult, None, [r_a, r_iv], [r_a])
            P.ts("dve", kk[:], ang[:], 1.0 / TWO_PI, None, ALU.mult, None, [r_a], [r_k])
            P.ts("dve", kk[:], kk[:], MAGIC, None, ALU.add, None, [r_k], [r_k])
            P.ts("dve", kk[:], kk[:], -MAGIC, None, ALU.add, None, [r_k], [r_k])
            P.stt("dve", rr[:], kk[:], -CW1, ang[:], ALU.mult, ALU.add, [r_k, r_a], [r_r])
            P.stt("dve", rr[:], kk[:], -CW2, rr[:], ALU.mult, ALU.add, [r_k, r_r], [r_r])
            P.stt("dve", rr[:], kk[:], -CW3, rr[:], ALU.mult, ALU.add, [r_k, r_r], [r_r])
            for shift, dst, r_dst in ((0.0, sinT, r_sin), (math.pi / 2, cosT, r_cos)):
                P.ts("dve", yy[:], rr[:], shift, None, ALU.add, None, [r_r], [r_y])
                P.ts("dve", mm_[:], yy[:], math.pi, -TWO_PI, ALU.is_gt, ALU.mult, [r_y], [r_m])
                P.tt("dve", yy[:], yy[:], mm_[:], ALU.add, [r_y, r_m], [r_y])
                P.ts("dve", mm_[:], yy[:], -math.pi, TWO_PI, ALU.is_lt, ALU.mult, [r_y], [r_m])
                P.tt("dve", yy[:], yy[:], mm_[:], ALU.add, [r_y, r_m], [r_y])
                P.ts("dve", yy[:], yy[:], -PI_SAFE, PI_SAFE, ALU.max, ALU.min, [r_y], [r_y])
                P.act(dst[:, sl], yy[:], AF.Sin, [r_y], [r_dst])
        ph.end()

    return cosT, r_cos, sinT, r_sin


def mla_latents(P, L):
    work = L["work"]
    win_sb = L["win_sb"]
    r_win = L["r_win"]
    winr = L["winr"]
    r_winr = L["r_winr"]
    x_ = L["x_"]
    r_x = L["r_x"]
    lat = L["lat"]
    r_lat = L["r_lat"]
    sqb = L["sqb"]
    psSq = L["psSq"]
    psSkv = L["psSkv"]
    ones5 = L["ones5"]
    r_15 = L["r_15"]
    psA = L["psA"]
    psB = L["psB"]
    ra = L["ra"]
    rb = L["rb"]
    cosT = L["cosT"]
    r_cos = L["r_cos"]
    sinT = L["sinT"]
    r_sin = L["r_sin"]
    tok = L["tok"]
    krs = L["krs"]
    r_krs = L["r_krs"]
    d_krs = L["d_krs"]
    KR = L["KR"]
    rq = L["rq"]
    r_rq = L["r_rq"]
    rkv = L["rkv"]
    r_rkv = L["r_rkv"]
    tq = L["tq"]
    r_tq = L["r_tq"]
    cqn = L["cqn"]
    r_cqn = L["r_cqn"]
    ckvn = L["ckvn"]
    r_ckvn = L["r_ckvn"]
    gq = L["gq"]
    r_gq = L["r_gq"]
    gkv = L["gkv"]
    r_gkv = L["r_gkv"]
    CQ_v = L["CQ_v"]
    CKV_v = L["CKV_v"]
    d_cqn = L["d_cqn"]
    d_ckvn = L["d_ckvn"]
    nw = L["nw"]
    nr = L["nr"]
    for e in range(8):
        ps, r_ps = work[nw % 4]
        nw += 1
        for c in range(NCH):
            P.mm(ps[:], win_sb[:, c, e * 128:(e + 1) * 128], x_[:, c, :], c == 0, c == NCH - 1,
                 [r_win, r_x], [r_ps])
        P.cp("dve", lat[:, e, :], ps[:], [r_ps], [r_lat[e]])
        s_, r_s = sqb[e % 2]
        P.act(s_[:], ps[:], AF.Square, [r_ps], [r_s])
        pst = psSq if e < 4 else psSkv
        P.mm(pst[0][:], ones5[:], s_[:], e % 4 == 0, e % 4 == 3, [r_15, r_s], [pst[1]])
    for c in range(NCH):
        P.mm(psA[0][0:64, :], win_sb[:, c, 1024:1088], x_[:, c, :], c == 0, c == NCH - 1,
             [r_win, r_x], [psA[1]])
    for c in range(NCH):
        P.mm(psB[0][0:64, :], winr[:, c, :], x_[:, c, :], c == 0, c == NCH - 1, [r_winr, r_x], [psB[1]])
    a_, r_a = ra[nr % 2]
    b_, r_b = rb[nr % 2]
    nr += 1
    P.tt("dve", a_[:], psA[0][0:64, :], cosT[:, tok], ALU.mult, [psA[1], r_cos], [r_a])
    P.tt("dve", b_[:], psB[0][0:64, :], sinT[:, tok], ALU.mult, [psB[1], r_sin], [r_b])
    P.tt("pool", krs[:], a_[:], b_[:], ALU.add, [r_a, r_b], [r_krs])
    P.dma("sp", KR[:, tok], krs[:], d_krs, reads=[r_krs])
    for pst, rdst, r_rdst in ((psSq, rq, r_rq), (psSkv, rkv, r_rkv)):
        P.ts("dve", tq[:], pst[0][:], RMS_EPS, None, ALU.add, None, [pst[1]], [r_tq])
        P.act(rdst[:], tq[:], AF.Sqrt, [r_tq], [r_rdst])
        P.op("dve", lambda e, rdst=rdst: e.reciprocal(rdst[:], rdst[:]), [r_rdst], [r_rdst])
    for e in range(4):
        P.stt("dve", cqn[:, e, :], lat[:, e, :], gq[:, e:e + 1], rq[:], ALU.mult, ALU.mult,
              [r_lat[e], r_gq, r_rq], [r_cqn])
    for e in range(4):
        P.stt("dve", ckvn[:, e, :], lat[:, 4 + e, :], gkv[:, e:e + 1], rkv[:], ALU.mult, ALU.mult,
              [r_lat[4 + e], r_gkv, r_rkv], [r_ckvn])

    P.dma("sp", CQ_v[:, :, tok], cqn[:], d_cqn, reads=[r_cqn])
    P.dma("sp", CKV_v[:, :, tok], ckvn[:], d_ckvn, reads=[r_ckvn])
    return nw, nr


def mla_part(P, nc, xT, tabs, w_in, qg, kvg, w_q, w_kv, maskc, kb_d, oT, QN, QR, KN, KR, V, CQ, CKV, first):
    xT_v = xT.rearrange("(c p) t -> p c t", p=128)
    win_v = w_in.rearrange("(c p) n -> p c n", p=128)
    wq_v = w_q.rearrange("(c p) (h e) -> p c h e", p=128, h=HG)
    wkv_v = w_kv.rearrange("(c p) (h t d) -> p c h t d", p=128, h=HG, t=2)
    QN_v = QN.rearrange("h p t -> p h t")
    QR_v = QR.rearrange("h p t -> p h t")
    KN_v = KN.rearrange("h p t -> p h t")
    V_v = V.rearrange("(n p) e -> p n e", p=128)
    oT_v = oT.rearrange("(h p) t -> p h t", p=128)
    scale = float((128 + 64) ** -0.5)

    with contextlib.ExitStack() as es0:
        ph0 = Ph(P, es0)
        cosT, r_cos, sinT, r_sin = tabs
        with contextlib.ExitStack() as es:
            ph = Ph(P, es)
            CQ_v = CQ.rearrange("e p t -> p e t")
            CKV_v = CKV.rearrange("e p t -> p e t")
            if first:
                win_sb, r_win = ph.sb([128, NCH, 1088], BF16)
                d_w = ph.dsem()
                for c4 in range(4):
                    P.dma("pool", win_sb[:, 4 * c4:4 * c4 + 4, :], win_v[:, 4 * c4:4 * c4 + 4, :], d_w,
                          writes=[r_win])
                winr, r_winr = ph.sb([128, NCH, 64], BF16)
                P.ts("dve", winr[:, :, 0:32], win_sb[:, :, 1056:1088], -1.0, None, ALU.mult, None, [r_win], [r_winr])
                P.cp("dve", winr[:, :, 32:64], win_sb[:, :, 1024:1056], [r_win], [r_winr])
            wqn, r_wqn = ph.sb([128, 4, HG, 128], BF16)
            wqr, r_wqr = ph.sb([128, 4, HG, 64], BF16)
            wqrot, r_wqrot = ph.sb([128, 4, HG, 64], BF16)
            wk, r_wk = ph.sb([128, 4, HG, 128], BF16)
            wv, r_wv = ph.sb([128, 4, HG, 128], BF16)
            gq, r_gq = ph.sb([128, 4], F32)
            gkv, r_gkv = ph.sb([128, 4], F32)
            P.dma("sp", gq[:], qg, ph.dsem(), writes=[r_gq])
            P.dma("sp", gkv[:], kvg, ph.dsem(), writes=[r_gkv])
            ones5, r_15 = ph.sb([128, 128], BF16)
            P.memset("pool", ones5[:], 1.0 / 512, [r_15])
            if first:
                xb = [ph.sb([128, NCH, 512], BF16) + (ph.dsem(),) for _ in range(2)]
                lat, r_lat = ph.sb([128, 8, 512], F32, nreg=8)
                sqb = [ph.sb([128, 512], BF16) for _ in range(2)]
            nsl = 1 if first else 2
            cq2 = [ph.sb([128, 4, 512], BF16) + (ph.dsem(),) for _ in range(nsl)]
            ckv2 = [ph.sb([128, 4, 512], BF16) + (ph.dsem(),) for _ in range(nsl)]
            rq, r_rq = ph.sb([128, 512], F32)
            rkv, r_rkv = ph.sb([128, 512], F32)
            tq, r_tq = ph.sb([128, 512], F32)
            ra = [ph.sb([64, 512], F32) for _ in range(2)]
            rb = [ph.sb([64, 512], F32) for _ in range(2)]
            krs, r_krs = ph.sb([64, 512], BF16)
            d_krs = ph.dsem()
            qns, r_qns = ph.sb([128, HG, 512], BF16)
            qrs, r_qrs = ph.sb([64, HG, 512], BF16)
            kns, r_kns = ph.sb([128, HG, 512], BF16)
            vs, r_vs = ph.sb([128, 4, HG * 128], BF16)
            d_qns, d_qrs, d_kns, d_vs = ph.dsem(), ph.dsem(), ph.dsem(), ph.dsem()
            work = [ph.ps() for _ in range(4)]
            psSq, psSkv = ph.ps(), ph.ps()
            psA, psB = ph.ps(), ph.ps()
            nw = 0
            nr = 0
            def load_x(tt):
                if first:
                    x_, r_x, d_x = xb[tt % 2]
                    P.dma("pool", x_[:], xT_v[:, :, tt * 512:(tt + 1) * 512], d_x, writes=[r_x])
                else:
                    c_, r_c, d_c = cq2[tt % nsl]
                    P.dma("sp", c_[:], CQ_v[:, :, tt * 512:(tt + 1) * 512], d_c, writes=[r_c])
                    c_, r_c, d_c = ckv2[tt % nsl]
                    P.dma("sp", c_[:], CKV_v[:, :, tt * 512:(tt + 1) * 512], d_c, writes=[r_c])

            load_x(0)
            d1, d2, d3, d4 = ph.dsem(), ph.dsem(), ph.dsem(), ph.dsem()
            for c in range(4):
                P.dma("pool", wqn[:, c, :, :], wq_v[:, c, :, 0:128], d1, writes=[r_wqn])
                P.dma("pool", wqr[:, c, :, :], wq_v[:, c, :, 128:192], d2, writes=[r_wqr])
                P.dma("pool", wk[:, c, :, :], wkv_v[:, c, :, 0, :], d3, writes=[r_wk])
                P.dma("pool", wv[:, c, :, :], wkv_v[:, c, :, 1, :], d4, writes=[r_wv])
            for c in range(4):
                P.ts("dve", wqrot[:, c, :, 0:32], wqr[:, c, :, 32:64], -1.0, None, ALU.mult, None, [r_wqr], [r_wqrot])
                P.cp("dve", wqrot[:, c, :, 32:64], wqr[:, c, :, 0:32], [r_wqr], [r_wqrot])
            for tt in range(S // 512):
                tok = slice(tt * 512, (tt + 1) * 512)
                cqn, r_cqn, d_cqn = cq2[tt % nsl]
                ckvn, r_ckvn, d_ckvn = ckv2[tt % nsl]
                if tt + 1 < S // 512:
                    load_x(tt + 1)
                if first:
                    x_, r_x, d_x = xb[tt % 2]
                    nw, nr = mla_latents(P, locals())
                for h in range(HG):
                    ps, r_ps = work[nw % 4]
                    nw += 1
                    for c in range(4):
                        P.mm(ps[:], wqn[:, c, h, :], cqn[:, c, :], c == 0, c == 3, [r_wqn, r_cqn], [r_ps])
                    P.act(qns[:, h, :], ps[:], AF.Identity, [r_ps], [r_qns])
                    for c in range(4):
                        P.mm(psA[0][0:64, :], wqr[:, c, h, :], cqn[:, c, :], c == 0, c == 3, [r_wqr, r_cqn], [psA[1]])
                    for c in range(4):
                        P.mm(psB[0][0:64, :], wqrot[:, c, h, :], cqn[:, c, :], c == 0, c == 3, [r_wqrot, r_cqn],
                             [psB[1]])
                    a_, r_a = ra[nr % 2]
                    b_, r_b = rb[nr % 2]
                    nr += 1
                    P.tt("dve", a_[:], psA[0][0:64, :], cosT[:, tok], ALU.mult, [psA[1], r_cos], [r_a])
                    P.tt("dve", b_[:], psB[0][0:64, :], sinT[:, tok], ALU.mult, [psB[1], r_sin], [r_b])
                    P.tt("pool", qrs[:, h, :], a_[:], b_[:], ALU.add, [r_a, r_b], [r_qrs])
                P.dma("sp", QN_v[:, :, tok], qns[:], d_qns, reads=[r_qns])
                P.dma("sp", QR_v[:, :, tok], qrs[:], d_qrs, reads=[r_qrs])
                for h in range(HG):
                    ps, r_ps = work[nw % 4]
                    nw += 1
                    for c in range(4):
                        P.mm(ps[:], wk[:, c, h, :], ckvn[:, c, :], c == 0, c == 3, [r_wk, r_ckvn], [r_ps])
                    P.act(kns[:, h, :], ps[:], AF.Identity, [r_ps], [r_kns])
                P.dma("sp", KN_v[:, :, tok], kns[:], d_kns, reads=[r_kns])
                for s in range(4):
                    for half in range(2):
                        ps, r_ps = work[nw % 4]
                        nw += 1
                        for c in range(4):
                            P.mm(ps[:], ckvn[:, c, s * 128:(s + 1) * 128],
                                 wv[:, c, half * 4:(half + 1) * 4, :], c == 0, c == 3, [r_ckvn, r_wv], [r_ps])
                        P.cp("dve", vs[:, s, half * 512:(half + 1) * 512], ps[:], [r_ps], [r_vs])
                P.dma("sp", V_v[:, tt * 4:(tt + 1) * 4, :], vs[:], d_vs, reads=[r_vs])
            ph.end()

        with contextlib.ExitStack() as es:
            ph = Ph(P, es)
            kr, r_kr = ph.sb([64, S], BF16)
            P.dma("sp", kr[:], KR, ph.dsem(), writes=[r_kr])
            slots = []
            for _ in range(2):
                slots.append(dict(qn=ph.sb([128, S], BF16) + (ph.dsem(),), qr=ph.sb([64, S], BF16) + (ph.dsem(),),
                                  kn=ph.sb([128, S], BF16) + (ph.dsem(),), v=ph.sb([128, S // 128, 128], BF16) + (ph.dsem(),)))

            def load_head(h, slot):
                sl = slots[slot]
                qn, r_qn, d_qn = sl["qn"]
                qr_, r_qr, d_qr = sl["qr"]
                kn, r_kn, d_kn = sl["kn"]
                v_, r_v, d_v = sl["v"]
                P.dma("sp", qn[:], QN[h], d_qn, writes=[r_qn])
                P.dma("sp", qr_[:], QR[h], d_qr, writes=[r_qr])
                P.dma("sp", kn[:], KN[h], d_kn, writes=[r_kn])
                P.dma("sp", v_[:], V_v[:, :, h * 128:(h + 1) * 128], d_v, writes=[r_v])
                parts = [
                    (lambda kt, kn=kn: kn[:, kt * 128:(kt + 1) * 128], lambda qt, qn=qn: qn[:, qt * 512:(qt + 1) * 512],
                     [r_kn, r_qn]),
                    (lambda kt: kr[:, kt * 128:(kt + 1) * 128], lambda qt, qr_=qr_: qr_[:, qt * 512:(qt + 1) * 512],
                     [r_kr, r_qr]),
                ]
                return dict(parts=parts, V=(v_, r_v))

            attn_core(P, ph, HG, load_head, scale, maskc, oT_v, kb_d)
            ph.end()
        ph0.end()


def mask_consts():
    kp = np.arange(128)[:, None, None]
    j = np.arange(4)[None, :, None]
    ql = np.arange(512)[None, None, :]
    causal = (ql >= 128 * j + kp)
    chunk = (ql // 64) >= ((128 * j + kp) // 64)
    return (np.ascontiguousarray(chunk.astype(np.float32).astype(ml_dtypes.bfloat16)),
            np.ascontiguousarray(causal.astype(np.float32).astype(ml_dtypes.bfloat16)))


def inv_freq64():
    f = (np.float32(10000.0) ** (-np.arange(0, 64, 2, dtype=np.float32) / np.float32(64))).astype(np.float32)
    return np.ascontiguousarray(np.concatenate([f, f])[:, None])


def fox_part(P, nc, xT, wq_d, wk_d, wv_d, wf_d, bfr, tri, elast, maskc, kb_d, oT, QF, KF, V, qt0=4):
    xT_v = xT.rearrange("(c p) t -> p c t", p=128)
    wq_v = wq_d.rearrange("(c p) n -> p c n", p=128)
    wk_v = wk_d.rearrange("(c p) n -> p c n", p=128)
    wv_v = wv_d.rearrange("(c p) n -> p c n", p=128)
    wf_v = wf_d.rearrange("(c p) n -> p c n", p=128)
    QF_v = QF.rearrange("h p t -> p h t")
    KF_v = KF.rearrange("h p t -> p h t")
    V_v = V.rearrange("(n p) e -> p n e", p=128)
    oT_v = oT.rearrange("(h p) t -> p h t", p=128)
    scale = float(128 ** -0.5)
    NTK = S // 128

    with contextlib.ExitStack() as es0:
        ph0 = Ph(P, es0)
        cneg, r_cneg = ph0.sb([128, NTK, HG], F32)
        cref, r_cref = ph0.sb([128, S // 512, HG], F32)
        with contextlib.ExitStack() as es:
            ph = Ph(P, es)
            wq, r_wq = ph.sb([128, NCH, HG * 128], BF16)
            wk, r_wk = ph.sb([128, NCH, HG * 128], BF16)
            wv, r_wv = ph.sb([128, NCH, HG * 128], BF16)
            wf, r_wf = ph.sb([128, NCH, HG], BF16)
            xb = [ph.sb([128, NCH, 512], BF16) + (ph.dsem(),) for _ in range(2)]
            P.dma("pool", xb[0][0][:], xT_v[:, :, 0:512], xb[0][2], writes=[xb[0][1]])
            for (t_, r_, v_) in ((wk, r_wk, wk_v), (wv, r_wv, wv_v)):
                d_ = ph.dsem()
                for c4 in range(4):
                    P.dma("pool", t_[:, 4 * c4:4 * c4 + 4, :], v_[:, 4 * c4:4 * c4 + 4, :], d_, writes=[r_])
            P.dma("pool", wf[:], wf_v, ph.dsem(), writes=[r_wf])
            d_ = ph.dsem()
            for c4 in range(4):
                P.dma("pool", wq[:, 4 * c4:4 * c4 + 4, :], wq_v[:, 4 * c4:4 * c4 + 4, :], d_, writes=[r_wq])
            bf_sb, r_bf = ph.sb([128, HG], F32)
            P.dma("sp", bf_sb[:], bfr, ph.dsem(), writes=[r_bf])
            tri_sb, r_tri = ph.sb([128, 128], F32)
            el_sb, r_el = ph.sb([128, 128], F32)
            P.dma("sp", tri_sb[:], tri, ph.dsem(), writes=[r_tri])
            P.dma("sp", el_sb[:], elast, ph.dsem(), writes=[r_el])
            fl, r_fl = ph.sb([128, NTK, HG], F32)
            qs, r_qs = ph.sb([128, HG, 512], BF16)
            ks, r_ks = ph.sb([128, HG, 512], BF16)
            vs, r_vs = ph.sb([128, 4, HG * 128], BF16)
            d_qs, d_ks, d_vs = ph.dsem(), ph.dsem(), ph.dsem()
            work = [ph.ps() for _ in range(6)]
            psF = ph.ps()
            nw = 0
            def load_x(tt):
                x_, r_x, d_x = xb[tt % 2]
                P.dma("pool", x_[:], xT_v[:, :, tt * 512:(tt + 1) * 512], d_x, writes=[r_x])

            for tt in range(S // 512):
                tok = slice(tt * 512, (tt + 1) * 512)
                x_, r_x, d_x = xb[tt % 2]
                if tt + 1 < S // 512:
                    load_x(tt + 1)
                for h in range(HG):
                    if tt >= qt0:
                        ps, r_ps = work[nw % 6]
                        nw += 1
                        for c in range(NCH):
                            P.mm(ps[:], wq[:, c, h * 128:(h + 1) * 128], x_[:, c, :], c == 0, c == NCH - 1,
                                 [r_wq, r_x], [r_ps])
                        P.act(qs[:, h, :], ps[:], AF.Identity, [r_ps], [r_qs])
                    ps, r_ps = work[nw % 6]
                    nw += 1
                    for c in range(NCH):
                        P.mm(ps[:], wk[:, c, h * 128:(h + 1) * 128], x_[:, c, :], c == 0, c == NCH - 1,
                             [r_wk, r_x], [r_ps])
                    P.cp("dve", ks[:, h, :], ps[:], [r_ps], [r_ks])
                if tt >= qt0:
                    P.dma("sp", QF_v[:, :, tok], qs[:], d_qs, reads=[r_qs])
                P.dma("sp", KF_v[:, :, tok], ks[:], d_ks, reads=[r_ks])
                for s in range(4):
                    for half in range(2):
                        ps, r_ps = work[nw % 6]
                        nw += 1
                        for c in range(NCH):
                            P.mm(ps[:], x_[:, c, s * 128:(s + 1) * 128], wv[:, c, half * 512:(half + 1) * 512],
                                 c == 0, c == NCH - 1, [r_x, r_wv], [r_ps])
                        if half == 0:
                            P.act(vs[:, s, 0:512], ps[:], AF.Identity, [r_ps], [r_vs])
                        else:
                            P.cp("dve", vs[:, s, 512:1024], ps[:], [r_ps], [r_vs])
                    for c in range(NCH):
                        P.mm(psF[0][:, 0:HG], x_[:, c, s * 128:(s + 1) * 128], wf[:, c, :], c == 0, c == NCH - 1,
                             [r_x, r_wf], [psF[1]])
                    P.tt("dve", fl[:, tt * 4 + s, :], psF[0][:, 0:HG], bf_sb[:], ALU.add, [psF[1], r_bf], [r_fl])
                P.dma("sp", V_v[:, tt * 4:(tt + 1) * 4, :], vs[:], d_vs, reads=[r_vs])
            P.act(fl[:], fl[:], AF.Exp, [r_fl], [r_fl], scale=-1.0)
            P.ts("dve", fl[:], fl[:], 1.0, None, ALU.add, None, [r_fl], [r_fl])
            P.act(fl[:], fl[:], AF.Ln, [r_fl], [r_fl])
            for i in range(NTK):
                P.mm(psF[0][:, 0:HG], tri_sb[:], fl[:, i, :], True, i == 0, [r_tri, r_fl], [psF[1]])
                if i > 0:
                    P.mm(psF[0][:, 0:HG], el_sb[:], cneg[:, i - 1, :], False, True, [r_el, r_cneg], [psF[1]])
                P.cp("dve", cneg[:, i, :], psF[0][:, 0:HG], [psF[1]], [r_cneg])
            for qt in range(S // 512):
                P.mm(psF[0][:, 0:HG], el_sb[:], cneg[:, 4 * qt + 1, :], True, True, [r_el, r_cneg], [psF[1]])
                P.cp("dve", cref[:, qt, :], psF[0][:, 0:HG], [psF[1]], [r_cref])
            ph.end()

        with contextlib.ExitStack() as es:
            ph = Ph(P, es)
            slots = []
            for _ in range(2):
                slots.append(dict(q=ph.sb([128, S], BF16) + (ph.dsem(),), k=ph.sb([128, S], BF16) + (ph.dsem(),),
                                  v=ph.sb([128, NTK, 128], BF16) + (ph.dsem(),)))

            def load_head(h, slot):
                sl = slots[slot]
                q_, r_q, d_q = sl["q"]
                k_, r_k, d_k = sl["k"]
                v_, r_v, d_v = sl["v"]
                P.dma("sp", q_[:, qt0 * 512:S], QF[h][:, qt0 * 512:S], d_q, writes=[r_q])
                P.dma("sp", k_[:], KF[h], d_k, writes=[r_k])
                P.dma("sp", v_[:], V_v[:, :, h * 128:(h + 1) * 128], d_v, writes=[r_v])
                parts = [(lambda kt, k_=k_: k_[:, kt * 128:(kt + 1) * 128],
                          lambda qt, q_=q_: q_[:, qt * 512:(qt + 1) * 512], [r_k, r_q])]
                return dict(parts=parts, V=(v_, r_v))

            attn_core(P, ph, HG, load_head, scale, maskc, oT_v, kb_d, fox=(cneg, r_cneg, cref, r_cref), qt0=qt0)
            ph.end()
        ph0.end()


def tri_consts():
    j = np.arange(128)[:, None]
    t = np.arange(128)[None, :]
    tri = (j <= t).astype(np.float32)
    el = np.zeros((128, 128), np.float32)
    el[127, :] = 1.0
    return np.ascontiguousarray(tri), el


def build_fused():
    nc = bass.Bass("TRN2", target_bir_lowering=False)
    inp = lambda n, sh, dt=F32: nc.dram_tensor(n, sh, dt, kind="ExternalInput").ap()
    xT = inp("xT", [D, S])
    posr = inp("posr", [64, S], I32)
    invf = inp("invf", [64, 1])
    mla_w_in = inp("mla_w_in", [D, 1088])
    qg = inp("qg", [128, 4])
    kvg = inp("kvg", [128, 4])
    mla_w_q = inp("mla_w_q", [512, 16 * 192])
    mla_w_kv = inp("mla_w_kv", [512, 16 * 256])
    mla_w_o = inp("mla_w_o", [D, D])
    fox_w_in = inp("fox_w_in", [D, 6160])
    bfr = inp("bfr", [128, 16])
    fox_w_o = inp("fox_w_o", [D, D])
    w_gu = [inp(f"w_gu{l}", [D, 2 * HID]) for l in range(2)]
    w_down = [inp(f"w_down{l}", [HID, D]) for l in range(2)]
    lnp = [inp(f"lnp{l}", [128, 64]) for l in range(2)]
    tri = inp("tri", [128, 128])
    elast = inp("elast", [128, 128])
    mchunk = inp("mchunk", [128, 4, 512], BF16)
    mcausal = inp("mcausal", [128, 4, 512], BF16)
    kb = inp("kb", [128, 32])
    H2 = S // 2
    yT = nc.dram_tensor("yT", [D, H2], F32, kind="ExternalOutput").ap()
    scr = lambda n, sh, dt=BF16: nc.dram_tensor(n, sh, dt).ap()
    QN, KN = scr("QN_s", [HG, 128, S]), scr("KN_s", [HG, 128, S])
    QR, KR = scr("QR_s", [HG, 64, S]), scr("KR_s", [64, S])
    V = scr("V_s", [S, HG * 128])
    CQ, CKV = scr("CQ_s", [4, 128, S]), scr("CKV_s", [4, 128, S])
    oT = scr("oT_s", [D, S])
    x1T = scr("x1T_s", [D, S], F32)
    xm = scr("xm_s", [D, S], F32)
    zs = scr("zs_s", [D, S], F32)
    W = HG * 128
    with contextlib.ExitStack() as es:
        P = Prog(nc, es)
        with contextlib.ExitStack() as es_t:
            pht = Ph(P, es_t)
            tabs = rope_tables(P, pht, posr, invf)
            for hg in range(2):
                mla_part(P, nc, xT, tabs, mla_w_in, qg, kvg,
                         mla_w_q[:, hg * HG * 192:(hg + 1) * HG * 192],
                         mla_w_kv[:, hg * HG * 256:(hg + 1) * HG * 256],
                         mchunk, kb, oT[hg * W:(hg + 1) * W, :], QN, QR, KN, KR, V, CQ, CKV, hg == 0)
            pht.end()
        dense_part(P, nc, oT, xT, mla_w_o, lnp[0], w_gu[0], w_down[0], x1T, xm, zs, nt=S, out_is_output=False)
        for hg in range(2):
            fox_part(P, nc, x1T, fox_w_in[:, hg * W:(hg + 1) * W], fox_w_in[:, 2048 + hg * W:2048 + (hg + 1) * W],
                     fox_w_in[:, 4096 + hg * W:4096 + (hg + 1) * W], fox_w_in[:, 6144 + hg * HG:6144 + (hg + 1) * HG],
                     bfr[:, hg * HG:(hg + 1) * HG], tri, elast, mcausal, kb, oT[hg * W:(hg + 1) * W, :], QN, KN, V)
        dense_part(P, nc, oT[:, H2:S], x1T[:, H2:S], fox_w_o, lnp[1], w_gu[1], w_down[1], yT, xm[:, 0:H2],
                   zs[:, 0:H2], nt=H2, out_is_output=True)
        P.finish()
    return nc


def kernel(x, positions, mla_w_in, mla_q_norm_g, mla_w_q_up, mla_kv_norm_g, mla_w_kv_up, mla_w_o,
           fox_w_in, fox_b_f, fox_w_o, ffn_w_gu, ffn_w_down, ln_mix_g, ln_mix_b, ln_ffn_g, ln_ffn_b):
    f = lambda a: np.ascontiguousarray(np.asarray(a, np.float32))
    x = f(x)
    positions = np.asarray(positions, np.int32)
    if "fused" not in _CACHE:
        _CACHE["fused"] = build_fused()
    nc = _CACHE["fused"]
    mchunk, mcausal = mask_consts()
    tri, el = tri_consts()
    common = {
        "invf": inv_freq64(), "mla_w_in": f(mla_w_in)[0],
        "qg": np.ascontiguousarray(f(mla_q_norm_g)[0].reshape(4, 128).T),
        "kvg": np.ascontiguousarray(f(mla_kv_norm_g)[0].reshape(4, 128).T),
        "mla_w_q": f(mla_w_q_up)[0], "mla_w_kv": f(mla_w_kv_up)[0], "mla_w_o": f(mla_w_o)[0],
        "fox_w_in": f(fox_w_in)[0],
        "bfr": np.ascontiguousarray(np.broadcast_to(f(fox_b_f)[0][None, :], (128, 16))),
        "fox_w_o": f(fox_w_o)[0],
        "w_gu0": f(ffn_w_gu)[0], "w_gu1": f(ffn_w_gu)[1], "w_down0": f(ffn_w_down)[0], "w_down1": f(ffn_w_down)[1],
        "lnp0": ln_params(f(ln_mix_g)[0], f(ln_mix_b)[0], f(ln_ffn_g)[0], f(ln_ffn_b)[0]),
        "lnp1": ln_params(f(ln_mix_g)[1], f(ln_mix_b)[1], f(ln_ffn_g)[1], f(ln_ffn_b)[1]),
        "tri": tri, "elast": el, "mchunk": mchunk, "mcausal": mcausal,
    }
    H2 = S // 2
    kb_nat = np.zeros((128, 32), np.float32)
    kb_rot = np.zeros((128, 32), np.float32)
    kb_rot[:, 0:16] = -30000.0
    in_maps = []
    for c in range(8):
        b, rot = c // 2, c % 2
        m = dict(common)
        xb_ = x[b]
        pb_ = positions[b]
        if rot:
            xb_ = np.concatenate([xb_[H2:], xb_[:H2]], axis=0)
            pb_ = np.concatenate([pb_[H2:], pb_[:H2]], axis=0)
        m["xT"] = np.ascontiguousarray(xb_.T)
        m["posr"] = np.ascontiguousarray(np.broadcast_to(pb_[None, :], (64, S)))
        m["kb"] = kb_rot if rot else kb_nat
        in_maps.append(m)
    res = run_bass_kernel_spmd(nc, in_maps, core_ids=list(range(8)))
    out = np.empty((B, S, D), np.float32)
    for c in range(8):
        b, rot = c // 2, c % 2
        if rot:
            out[b, :H2] = res.results[c]["yT"].T
        else:
            out[b, H2:] = res.results[c]["yT"].T
    return out
```
